# Optimizing a Trainium2 kernel written in Bass

```python
import math
import jax, jax.numpy as jnp
from jax import lax
import numpy as np


D_MODEL = 1024
BATCH = 8
SEQ = 2048
DEPTH = 1
DEC_BATCH = 16
DEC_SEQ = 64
PAST_LEN = 4096

CHUNK = 64
PLE_DIM = 256
SB_WIDTH = D_MODEL // 2
SB_HEAD_DIM = 64
SB_HEADS = SB_WIDTH // SB_HEAD_DIM
SB_BLOCK = 128
SSM_WIDTH = D_MODEL // 2
SSM_GROUP = 16
SSM_GROUPS = SSM_WIDTH // SSM_GROUP
SSM_STATE = 64
D_FF = -(-8 * D_MODEL // (3 * 256)) * 256
IN_WIDTH = 3 * SB_WIDTH + SSM_WIDTH + 2 * D_MODEL
RMS_EPS = 1e-6

kernel_name = 'stickbreak_s5_streaming_step'


def _rmsnorm(x, gain):
    xf = x.astype(jnp.float32)
    y = xf * lax.rsqrt(jnp.mean(xf * xf, axis=-1, keepdims=True) + RMS_EPS)
    return (y * gain.astype(jnp.float32)).astype(x.dtype)


def _sb_block(qb, start, k, v, q_offset):
    z = jnp.einsum('bqhd,bkhd->bhqk', qb, k) * (SB_HEAD_DIM ** -0.5)
    t_pos = q_offset + start + jnp.arange(qb.shape[1])
    s_pos = jnp.arange(k.shape[1])
    visible = s_pos[None, :] < t_pos[:, None]
    log_keep = jnp.where(visible, jax.nn.log_sigmoid(-z), 0.0)
    after = lax.cumsum(log_keep, axis=3, reverse=True) - log_keep
    w = jnp.where(visible, jnp.exp(jax.nn.log_sigmoid(z) + after), 0.0)
    return jnp.einsum('bhqk,bkhd->bqhd', w, v)


def _stick_breaking(q, k, v, q_offset):
    bsz, lq = q.shape[0], q.shape[1]
    blk = min(SB_BLOCK, lq)
    nb = lq // blk
    qf = q.astype(jnp.float32).reshape(bsz, nb, blk, SB_HEADS, SB_HEAD_DIM)
    qf = jnp.moveaxis(qf, 1, 0)
    kf = k.astype(jnp.float32)
    vf = v.astype(jnp.float32)
    starts = jnp.arange(nb) * blk
    out = lax.map(lambda a: _sb_block(a[0], a[1], kf, vf, q_offset), (qf, starts))
    return jnp.moveaxis(out, 0, 1).reshape(bsz, lq, SB_HEADS, SB_HEAD_DIM)


def _s5(u, s_re0, s_im0, a_re, a_im, log_dt, b_re, b_im, c_re, c_im, d, w_glu):
    bsz, L, _ = u.shape
    uf = u.astype(jnp.float32).reshape(bsz, L, SSM_GROUPS, SSM_GROUP)
    a_re = a_re.astype(jnp.float32)
    a_im = a_im.astype(jnp.float32)
    dt = jnp.exp(log_dt.astype(jnp.float32))[:, None]
    mag = jnp.exp(a_re * dt)
    lb_re = mag * jnp.cos(a_im * dt)
    lb_im = mag * jnp.sin(a_im * dt)
    den = a_re * a_re + a_im * a_im
    nr, ni = lb_re - 1.0, lb_im
    f_re = (nr * a_re + ni * a_im) / den
    f_im = (ni * a_re - nr * a_im) / den
    b_re = b_re.astype(jnp.float32)
    b_im = b_im.astype(jnp.float32)
    bb_re = f_re[:, :, None] * b_re - f_im[:, :, None] * b_im
    bb_im = f_re[:, :, None] * b_im + f_im[:, :, None] * b_re
    bu_re = jnp.einsum('gpc,btgc->btgp', bb_re, uf)
    bu_im = jnp.einsum('gpc,btgc->btgp', bb_im, uf)
    s_re0 = s_re0.astype(jnp.float32)
    s_im0 = s_im0.astype(jnp.float32)
    bu_re = bu_re.at[:, 0].add(lb_re * s_re0 - lb_im * s_im0)
    bu_im = bu_im.at[:, 0].add(lb_re * s_im0 + lb_im * s_re0)
    ar = jnp.broadcast_to(lb_re, bu_re.shape)
    ai = jnp.broadcast_to(lb_im, bu_im.shape)

    def combine(e1, e2):
        ar1, ai1, br1, bi1 = e1
        ar2, ai2, br2, bi2 = e2
        return (ar2 * ar1 - ai2 * ai1,
                ar2 * ai1 + ai2 * ar1,
                ar2 * br1 - ai2 * bi1 + br2,
                ar2 * bi1 + ai2 * br1 + bi2)

    _, _, s_re, s_im = lax.associative_scan(combine, (ar, ai, bu_re, bu_im), axis=1)
    y = (jnp.einsum('gcp,btgp->btgc', c_re.astype(jnp.float32), s_re)
         - jnp.einsum('gcp,btgp->btgc', c_im.astype(jnp.float32), s_im)
         + d.astype(jnp.float32).reshape(SSM_GROUPS, SSM_GROUP) * uf)
    y = jax.nn.gelu(y.reshape(bsz, L, SSM_WIDTH))
    y = y * jax.nn.sigmoid(y @ w_glu.astype(jnp.float32))
    return y.astype(u.dtype), s_re[:, -1], s_im[:, -1]


def _layer(x, p, k_past, v_past, s_re0, s_im0, W):
    bsz, L, _ = x.shape
    h = _rmsnorm(x, W['norm_mix_pre'])
    proj = h @ W['w_in']
    q, k, v, u, g_attn, g_ssm = jnp.split(
        proj, [SB_WIDTH, 2 * SB_WIDTH, 3 * SB_WIDTH, 3 * SB_WIDTH + SSM_WIDTH,
               3 * SB_WIDTH + SSM_WIDTH + D_MODEL], axis=-1)
    q = q.reshape(bsz, L, SB_HEADS, SB_HEAD_DIM)
    k = k.reshape(bsz, L, SB_HEADS, SB_HEAD_DIM)
    v = v.reshape(bsz, L, SB_HEADS, SB_HEAD_DIM)
    if k_past is None:
        k_all, v_all, q_offset = k, v, 0
    else:
        k_all = jnp.concatenate([k_past, k.astype(k_past.dtype)], axis=1)
        v_all = jnp.concatenate([v_past, v.astype(v_past.dtype)], axis=1)
        q_offset = k_past.shape[1]
    o_attn = _stick_breaking(q, k_all, v_all, q_offset).reshape(bsz, L, SB_WIDTH).astype(x.dtype)
    o_ssm, s_re, s_im = _s5(u, s_re0, s_im0, W['ssm_a_re'], W['ssm_a_im'], W['ssm_log_dt'],
                            W['ssm_b_re'], W['ssm_b_im'], W['ssm_c_re'], W['ssm_c_im'],
                            W['ssm_d'], W['w_glu'])
    merged = (jax.nn.sigmoid(g_attn) * (o_attn @ W['w_branch_attn'])
              + jax.nn.sigmoid(g_ssm) * (o_ssm @ W['w_branch_ssm']))
    x = x + _rmsnorm(merged @ W['w_out'], W['norm_mix_post'])
    f = _rmsnorm(x, W['norm_ffn_pre'])
    f = (jax.nn.silu(f @ W['w_ffn_gate']) * (f @ W['w_ffn_up'])) @ W['w_ffn_down']
    x = x + _rmsnorm(f, W['norm_ffn_post'])
    gate = jax.nn.sigmoid(_rmsnorm(x, W['norm_ple_pre']) @ W['w_ple_gate'])
    pe = gate * (p @ W['w_ple_proj'])
    x = x + _rmsnorm(pe, W['norm_ple_post'])
    return x, k, v, s_re, s_im


def setup_inputs(seed: int = 0) -> dict:
    key = jax.random.key(seed)
    ks = iter(jax.random.split(key, 40))
    f32 = jnp.float32

    def nrm(shape, fan_in):
        return jax.random.normal(next(ks), shape, f32) * (fan_in ** -0.5)

    def gain(shape):
        return 1.0 + 0.05 * jax.random.normal(next(ks), shape, f32)

    a_im_base = math.pi * jnp.arange(SSM_STATE, dtype=f32)
    return {
        'x_prompt': jax.random.normal(next(ks), (BATCH, SEQ, D_MODEL), f32),
        'x_sample': jax.random.normal(next(ks), (DEC_BATCH, DEC_SEQ, D_MODEL), f32),
        'cache_k': jax.random.normal(next(ks), (DEPTH, DEC_BATCH, PAST_LEN, SB_HEADS, SB_HEAD_DIM), f32),
        'cache_v': jax.random.normal(next(ks), (DEPTH, DEC_BATCH, PAST_LEN, SB_HEADS, SB_HEAD_DIM), f32),
        'state_ssm_re': 0.3 * jax.random.normal(next(ks), (DEPTH, DEC_BATCH, SSM_GROUPS, SSM_STATE), f32),
        'state_ssm_im': 0.3 * jax.random.normal(next(ks), (DEPTH, DEC_BATCH, SSM_GROUPS, SSM_STATE), f32),
        'p_prompt': jax.random.normal(next(ks), (DEPTH, BATCH, SEQ, PLE_DIM), f32),
        'p_sample': jax.random.normal(next(ks), (DEPTH, DEC_BATCH, DEC_SEQ, PLE_DIM), f32),
        'norm_mix_pre': gain((DEPTH, D_MODEL)),
        'norm_mix_post': gain((DEPTH, D_MODEL)),
        'w_in': nrm((DEPTH, D_MODEL, IN_WIDTH), D_MODEL),
        'ssm_a_re': -0.5 + 0.01 * jax.random.normal(next(ks), (DEPTH, SSM_GROUPS, SSM_STATE), f32),
        'ssm_a_im': a_im_base + 0.01 * jax.random.normal(next(ks), (DEPTH, SSM_GROUPS, SSM_STATE), f32),
        'ssm_log_dt': jax.random.uniform(next(ks), (DEPTH, SSM_GROUPS), f32,
                                         minval=math.log(1e-3), maxval=math.log(1e-1)),
        'ssm_b_re': nrm((DEPTH, SSM_GROUPS, SSM_STATE, SSM_GROUP), SSM_GROUP),
        'ssm_b_im': nrm((DEPTH, SSM_GROUPS, SSM_STATE, SSM_GROUP), SSM_GROUP),
        'ssm_c_re': nrm((DEPTH, SSM_GROUPS, SSM_GROUP, SSM_STATE), SSM_STATE),
        'ssm_c_im': nrm((DEPTH, SSM_GROUPS, SSM_GROUP, SSM_STATE), SSM_STATE),
        'ssm_d': jax.random.normal(next(ks), (DEPTH, SSM_WIDTH), f32),
        'w_glu': nrm((DEPTH, SSM_WIDTH, SSM_WIDTH), SSM_WIDTH),
        'w_branch_attn': nrm((DEPTH, SB_WIDTH, D_MODEL), SB_WIDTH),
        'w_branch_ssm': nrm((DEPTH, SSM_WIDTH, D_MODEL), SSM_WIDTH),
        'w_out': nrm((DEPTH, D_MODEL, D_MODEL), D_MODEL),
        'norm_ffn_pre': gain((DEPTH, D_MODEL)),
        'norm_ffn_post': gain((DEPTH, D_MODEL)),
        'w_ffn_gate': nrm((DEPTH, D_MODEL, D_FF), D_MODEL),
        'w_ffn_up': nrm((DEPTH, D_MODEL, D_FF), D_MODEL),
        'w_ffn_down': nrm((DEPTH, D_FF, D_MODEL), D_FF),
        'norm_ple_pre': gain((DEPTH, D_MODEL)),
        'norm_ple_post': gain((DEPTH, D_MODEL)),
        'w_ple_gate': nrm((DEPTH, D_MODEL, D_MODEL), D_MODEL),
        'w_ple_proj': nrm((DEPTH, PLE_DIM, D_MODEL), PLE_DIM),
    }


def reference(x_prompt, x_sample, cache_k, cache_v, state_ssm_re, state_ssm_im, p_prompt, p_sample,
              norm_mix_pre, norm_mix_post, w_in, ssm_a_re, ssm_a_im, ssm_log_dt, ssm_b_re, ssm_b_im,
              ssm_c_re, ssm_c_im, ssm_d, w_glu, w_branch_attn, w_branch_ssm, w_out,
              norm_ffn_pre, norm_ffn_post, w_ffn_gate, w_ffn_up, w_ffn_down,
              norm_ple_pre, norm_ple_post, w_ple_gate, w_ple_proj):
    yp, ys = x_prompt, x_sample
    kp_l, vp_l, srp_l, sip_l = [], [], [], []
    ks_l, vs_l, srs_l, sis_l = [], [], [], []
    for i in range(DEPTH):
        W = dict(norm_mix_pre=norm_mix_pre[i], norm_mix_post=norm_mix_post[i], w_in=w_in[i],
                 ssm_a_re=ssm_a_re[i], ssm_a_im=ssm_a_im[i], ssm_log_dt=ssm_log_dt[i],
                 ssm_b_re=ssm_b_re[i], ssm_b_im=ssm_b_im[i], ssm_c_re=ssm_c_re[i], ssm_c_im=ssm_c_im[i],
                 ssm_d=ssm_d[i], w_glu=w_glu[i], w_branch_attn=w_branch_attn[i],
                 w_branch_ssm=w_branch_ssm[i], w_out=w_out[i],
                 norm_ffn_pre=norm_ffn_pre[i], norm_ffn_post=norm_ffn_post[i],
                 w_ffn_gate=w_ffn_gate[i], w_ffn_up=w_ffn_up[i], w_ffn_down=w_ffn_down[i],
                 norm_ple_pre=norm_ple_pre[i], norm_ple_post=norm_ple_post[i],
                 w_ple_gate=w_ple_gate[i], w_ple_proj=w_ple_proj[i])
        zero_state = jnp.zeros((x_prompt.shape[0], SSM_GROUPS, SSM_STATE), jnp.float32)
        yp, kp, vp, srp, sip = _layer(yp, p_prompt[i], None, None, zero_state, zero_state, W)
        ys, kn, vn, srs, sis = _layer(ys, p_sample[i], cache_k[i], cache_v[i],
                                      state_ssm_re[i], state_ssm_im[i], W)
        kp_l.append(kp); vp_l.append(vp); srp_l.append(srp); sip_l.append(sip)
        ks_l.append(kn); vs_l.append(vn); srs_l.append(srs); sis_l.append(sis)
    k_prompt = jnp.stack(kp_l)
    v_prompt = jnp.stack(vp_l)
    ssm_re_prompt = jnp.stack(srp_l)
    ssm_im_prompt = jnp.stack(sip_l)
    k_sample = jnp.stack(ks_l)
    v_sample = jnp.stack(vs_l)
    ssm_re_sample = jnp.stack(srs_l)
    ssm_im_sample = jnp.stack(sis_l)
    return (yp, ys, k_prompt, v_prompt, ssm_re_prompt, ssm_im_prompt,
            k_sample, v_sample, ssm_re_sample, ssm_im_sample)
```

```python
from contextlib import ExitStack
import math
import os
import numpy as np
import concourse.bass as bass
import concourse.mybir as mybir
from concourse.bass_utils import run_bass_kernel_spmd

F32 = mybir.dt.float32
BF16 = mybir.dt.bfloat16
I32 = mybir.dt.int32
AF = mybir.ActivationFunctionType
ALU = mybir.AluOpType

D = 1024
SEQ = 2048
NH = 8
HD = 64
DFF = 2816
PLE = 256
PAST = 4096
DSEQ = 64
TT = 256
NCORES = 8
EPS = 1e-6

ENGINES = ("pe", "act", "dve", "pool", "sp")
SAME_ENG_WINDOW = 10 ** 9
NDMA_RING = {"pe": 1, "act": 8, "dve": 1, "pool": 16, "sp": 12}


class Op:
    __slots__ = ("eng", "fn", "deps", "inc", "count", "is_dma", "sem", "target", "ring_prev", "name", "idx")

    def __init__(self, eng, fn, is_dma, name):
        self.eng = eng
        self.fn = fn
        self.is_dma = is_dma
        self.deps = []
        self.inc = False
        self.count = None
        self.sem = None
        self.target = None
        self.ring_prev = None
        self.name = name
        self.idx = -1


class Sched:
    def __init__(self):
        self.ops = {e: [] for e in ENGINES}
        self.lastw = {}
        self.readers = {}
        self.dma_count = {e: 0 for e in ENGINES}
        self.dma_hist = {e: [] for e in ENGINES}

    def add(self, eng, fn, reads=(), writes=(), dma=False, name=""):
        op = Op(eng, fn, dma, name)
        op.idx = len(self.ops[eng])
        deps = []
        seen = set()

        def push(d, force=False):
            if d is None or id(d) in seen:
                return
            if (not d.is_dma) and d.eng == eng and eng != "pe" and op.idx - d.idx <= SAME_ENG_WINDOW:
                force = True
            if d.is_dma or dma or d.eng != eng or force:
                seen.add(id(d))
                deps.append(d)

        for k in reads:
            k0 = k[0] if isinstance(k, tuple) else k
            w = self.lastw.get(k)
            near = (w is not None and not w.is_dma and w.eng == eng and eng != "pe" and op.idx - w.idx <= SAME_ENG_WINDOW)
            push(w, force=near or (isinstance(k0, str) and k0.startswith("s:")))
        for k in writes:
            push(self.lastw.get(k))
            for r in self.readers.get(k, ()):
                push(r)
        op.deps = deps
        for d in deps:
            if not d.is_dma:
                d.inc = True
        for k in reads:
            self.readers.setdefault(k, []).append(op)
        for k in writes:
            self.lastw[k] = op
            self.readers[k] = []
        if dma:
            i = self.dma_count[eng]
            self.dma_count[eng] = i + 1
            nr = NDMA_RING[eng]
            op.sem = (eng, i % nr)
            op.target = 16 * (i // nr + 1)
            hist = self.dma_hist[eng]
            if i >= nr:
                op.ring_prev = hist[i - nr]
            hist.append(op)
        self.ops[eng].append(op)
        return op

    def pending(self, keys):
        out = []
        for k in keys:
            w = self.lastw.get(k)
            if w is not None:
                out.append(w)
            out.extend(self.readers.get(k, ()))
        return out

    def barrier_wait(self, eng, ops, name="barrier"):
        op = Op(eng, None, False, name)
        op.deps = [d for d in ops if d.is_dma or d.eng != eng]
        for d in op.deps:
            if not d.is_dma:
                d.inc = True
        self.ops[eng].append(op)
        return op

    def assign(self):
        for e in ENGINES:
            c = 0
            for op in self.ops[e]:
                if op.is_dma:
                    continue
                if op.inc:
                    c += 1
                op.count = c

    def emit_engine(self, e, h, sems, dma_sems):
        waited = {f: 0 for f in ENGINES}
        dwaited = {}
        for op in self.ops[e]:
            dl = list(op.deps)
            if op.ring_prev is not None:
                dl.append(op.ring_prev)
            for d in dl:
                if d.is_dma:
                    if dwaited.get(d.sem, 0) >= d.target:
                        continue
                    dwaited[d.sem] = d.target
                    h.wait_ge(dma_sems[d.sem], d.target)
                else:
                    if waited[d.eng] >= d.count:
                        continue
                    waited[d.eng] = d.count
                    h.wait_ge(sems[d.eng], d.count)
            if op.fn is None:
                continue
            ins = op.fn(h)
            if op.is_dma:
                ins.then_inc(dma_sems[op.sem], 16)
            elif op.inc:
                assert ins is not None, op.name
                ins.then_inc(sems[e], 1)


def weight_blocks():
    blks = []

    def add(name, w, kc0, kcn, c0, ncols):
        blks.append(dict(name=name, w=w, kc0=kc0, kcn=kcn, c0=c0, ncols=ncols))

    for nm, base in (("q", 0), ("k", 512), ("v", 1024), ("u", 1536)):
        for i in range(2):
            add(f"{nm}{i}", "w_in", 0, 8, base + 256 * i, 256)
    add("glu", "w_glu", 0, 4, 0, 512)
    for half in range(2):
        p0 = 2 * half
        add(f"ga{p0}", "w_in", 0, 8, 2048 + 256 * p0, 256)
        add(f"gs{p0}", "w_in", 0, 8, 3072 + 256 * p0, 256)
        add(f"wba{half}", "w_ba", 0, 4, 512 * half, 512)
        add(f"wbs{half}", "w_bs", 0, 4, 512 * half, 512)
        add(f"ga{p0 + 1}", "w_in", 0, 8, 2048 + 256 * (p0 + 1), 256)
        add(f"gs{p0 + 1}", "w_in", 0, 8, 3072 + 256 * (p0 + 1), 256)
    for i in range(4):
        add(f"wo{i}", "w_out", 0, 8, 256 * i, 256)
    for i in range(11):
        add(f"fg{i}", "w_fg", 0, 8, 256 * i, 256)
        add(f"fu{i}", "w_fu", 0, 8, 256 * i, 256)
    for cb in range(4):
        for kg, (k0, kn) in enumerate(((0, 8), (8, 8), (16, 6))):
            add(f"fd{cb}_{kg}", "w_fd", k0, kn, 256 * cb, 256)
    for i in range(4):
        add(f"pg{i}", "w_pg", 0, 8, 256 * i, 256)
    add("pp", "w_pp", 0, 2, 0, 1024)
    return blks


WBLKS = weight_blocks()
WIDX = {b["name"]: i for i, b in enumerate(WBLKS)}
NWB = len(WBLKS)
NSLOT = 6
LOOKAHEAD = 2

W_SHAPES = dict(w_in=(1024, 4096), w_glu=(512, 512), w_ba=(512, 1024), w_bs=(512, 1024),
                w_out=(1024, 1024), w_fg=(1024, 2816), w_fu=(1024, 2816), w_fd=(2816, 1024),
                w_pg=(1024, 1024), w_pp=(256, 1024))


class _Stop(Exception):
    pass


def build_program(NT=8, with_sample=True, dbg=(), stop=None):
    nc = bass.Bass("TRN2", target_bir_lowering=False, dynamic_dma_scratch_size=4096)
    S = Sched()
    es = ExitStack()

    def din(name, shape, dt=F32):
        return nc.dram_tensor(name, list(shape), dt, kind="ExternalInput").ap()

    def dout(name, shape, dt=F32):
        return nc.dram_tensor(name, list(shape), dt, kind="ExternalOutput").ap()

    xp = din("xp", [SEQ, D])
    xs = din("xs", [128, D])
    pp_in = din("pp", [SEQ, PLE])
    ps_in = din("psm", [128, PLE])
    ck = din("ck", [2, PAST, 512])
    cv = din("cv", [2, PAST, 512])
    s0 = din("s0", [128, 2, 2, 16])
    wd = {k: din(k, v) for k, v in W_SHAPES.items()}
    gains_in = din("gains", [128, 48])
    ssmn_in = din("ssm_n", [128, 3, 16])
    bpad_in = din("bpad", [128, 2, 4, 512])
    cpad_in = din("cpad", [128, 2, 16, 128])
    dn_in = din("d_n", [128, 4])
    cst_in = din("cst", [128, 4, 128])
    wscr = {k: nc.dram_tensor("wscr_" + k, list(v), BF16, kind="Internal").ap() for k, v in W_SHAPES.items()}

    yp = dout("yp", [SEQ, D])
    ys = dout("ys", [128, D])
    kp = dout("kp", [SEQ, 512])
    vp = dout("vp", [SEQ, 512])
    ks = dout("ks", [128, 512])
    vs = dout("vs", [128, 512])
    stp = dout("stp", [128, 16, 2])
    sts = dout("sts", [128, 2, 16, 2])
    dbg_out = {}
    out_dmas = []

    def sb(name, shape, dt=F32):
        return es.enter_context(nc.sbuf_tensor(name, list(shape), dt))

    cst_f = sb("cst_f", [128, 4, 128])
    ident_f = cst_f[:, 0, :]
    tmask_f = cst_f[:, 3, :]
    ones_f = sb("ones_f", [128, 128])
    cb_b = sb("cb_b", [128, 4, 128], BF16)
    ident_b = cb_b[:, 0, :]
    tri_b = cb_b[:, 1, :]
    nones_b = cb_b[:, 2, :]
    mb_b = cb_b[:, 3, :]
    ones_b = sb("ones_b", [128, 128], BF16)
    zeros_b = sb("zeros_b", [128, 64], BF16)
    gains = sb("gains_sb", [128, 48])
    d_n = sb("d_n_sb", [128, 4])
    ssmn = sb("ssmn_sb", [128, 3, 16])
    sm = sb("sm", [128, 24, 16])
    smi = sb("smi", [128, 16], I32)
    cs = sb("cs", [128, 16, 2, 128])
    rho_b = sb("rho_b", [128, 16, 128])
    bbT = sb("bbT", [128, 4, 2, 512], BF16)
    Cp = sb("Cp", [128, 16, 2, 128], BF16)
    carry = sb("carry", [128, 16, 2])
    carry_s = sb("carry_s", [128, 2, 16, 2])
    s0_sb = sb("s0_sb", [128, 2, 2, 16])
    ctmp = sb("ctmp", [128, 4])
    csn = sb("csn", [128, 2, 16])
    xin = sb("xin", [128, 2, 1024])
    xT = sb("xT", [128, 8, TT])
    fout = sb("fout", [128, 8, TT])
    kvtok = sb("kvtok", [128, 2, 1024])
    nT = sb("nT", [128, 8, TT], BF16)
    mg = sb("mg", [128, 8, TT], BF16)
    qz = sb("qz", [128, 4, 2, TT], BF16)
    kTn = sb("kTn", [128, 4, 128], BF16)
    uT = sb("uT", [128, 4, TT], BF16)
    du = sb("du", [128, 4, TT])
    pst = du[:, 0:2, :]
    o_att = sb("o_att", [128, 4, TT], BF16)
    o_ssm = sb("o_ssm", [128, 4, TT], BF16)
    sig = sb("sig", [128, 3, TT])
    rstd = sb("rstd", [128, TT])
    NR = 25
    R = sb("R", [128, NR, TT])
    kT_all = sb("kT_all", [128, 4 * SEQ], BF16)
    v_all = sb("v_all", [128, 16 * 512], BF16)
    wsl = sb("wsl", [128, NSLOT, 2048], BF16)
    pT = sb("pT", [128, 2, TT], BF16)

    pbank = [es.enter_context(nc.psum_tensor(f"pb{i}", [128, 512], F32)) for i in range(8)]

    kT3 = kT_all[:].rearrange("p (c t) -> p c t", c=4)
    v3 = v_all[:].rearrange("p (b f) -> p b f", f=512)

    def Rf(i, n=TT):
        return R[:, i, 0:n]

    def Rb(i):
        return R[:, i, :].bitcast(BF16)

    def Rf2(i):
        return R[:, i:i + 2, :].rearrange("p a t -> p (a t)")

    gctr = [0]
    reserved = set()

    def galloc():
        while True:
            b = gctr[0] % 8
            gctr[0] += 1
            if b not in reserved:
                return pbank[b][:, 0:256], ("PB", b)

    def dma(q, out, in_, reads, writes, name=""):
        return S.add(q, lambda e: e.dma_start(out=out, in_=in_), reads=reads, writes=writes, dma=True, name=name)

    def mm_group(out_ap, pairs, reads, writes, name="", start=True, stop=True):
        def fn(e):
            ins = None
            n = len(pairs)
            for i, (l, r) in enumerate(pairs):
                ins = e.matmul(out_ap, lhsT=l, rhs=r, start=(start and i == 0), stop=(stop and i == n - 1))
            return ins
        return S.add("pe", fn, reads=reads, writes=writes, name=name)

    def act(out, in_, func, reads, writes, scale=None, bias=None, name=""):
        kw = {}
        if scale is not None:
            kw["scale"] = scale
        if bias is not None:
            kw["bias"] = bias
        return S.add("act", lambda e: e.activation(out=out, in_=in_, func=func, **kw), reads=reads, writes=writes, name=name)

    def tt(eng, out, in0, in1, op, reads, writes, name=""):
        return S.add(eng, lambda e: e.tensor_tensor(out=out, in0=in0, in1=in1, op=op), reads=reads, writes=writes, name=name)

    def ts(eng, out, in0, s1, op0, reads, writes, s2=None, op1=None, name=""):
        if op1 is None:
            return S.add(eng, lambda e: e.tensor_scalar(out=out, in0=in0, scalar1=s1, scalar2=None, op0=op0), reads=reads, writes=writes, name=name)
        return S.add(eng, lambda e: e.tensor_scalar(out=out, in0=in0, scalar1=s1, scalar2=s2, op0=op0, op1=op1), reads=reads, writes=writes, name=name)

    def stt(out, in0, scalar, in1, op0, op1, reads, writes, name=""):
        return S.add("dve", lambda e: e.scalar_tensor_tensor(out=out, in0=in0, scalar=scalar, in1=in1, op0=op0, op1=op1), reads=reads, writes=writes, name=name)

    def copy(eng, out, in_, reads, writes, name=""):
        if eng == "act":
            return act(out, in_, AF.Copy, reads, writes, name=name)
        return S.add(eng, lambda e: e.tensor_copy(out=out, in_=in_), reads=reads, writes=writes, name=name)

    def memset(eng, ap, val, writes):
        return S.add(eng, lambda e: e.memset(ap, val), writes=writes)

    def tap(name, ap, shape, reads):
        if name not in dbg:
            return
        d = dout("dbg_" + name, shape)
        dbg_out[name] = d
        out_dmas.append(dma("pool", d, ap, reads, [], name="dbg_" + name))

    def checkpoint(name):
        if stop == name:
            raise _Stop()

    evt = [0]

    def evac_eng():
        evt[0] += 1
        return "act" if evt[0] % 2 else "dve"

    try:
        cvt_parts = {}
        for wname in ("w_in", "w_glu", "w_ba", "w_bs", "w_out", "w_fg", "w_fu", "w_fd", "w_pg", "w_pp"):
            rows, cols = W_SHAPES[wname]
            parts = []
            c0 = 0
            while c0 < cols:
                cn = min(2048, cols - c0)
                parts.append((c0, cn))
                c0 += cn
            cvt_parts[wname] = parts
            for pi, (c0, cn) in enumerate(parts):
                dma("pool", wscr[wname][:, c0:c0 + cn], wd[wname][:, c0:c0 + cn], [], [("wscr", wname, pi)], name="cvt")

        wstate = dict(next_load=0)

        def wuse(tile_i, name):
            gi = tile_i * NWB + WIDX[name]
            total = ntiles_total * NWB
            while wstate["next_load"] <= min(gi + LOOKAHEAD, total - 1):
                g = wstate["next_load"]
                li = g % NWB
                b = WBLKS[li]
                n = b["kcn"] * b["ncols"]
                src = wscr[b["w"]].rearrange("(kc p) n -> p kc n", p=128)[:, b["kc0"]:b["kc0"] + b["kcn"], b["c0"]:b["c0"] + b["ncols"]]
                rk = [("wscr", b["w"], pi) for pi, (pc0, pcn) in enumerate(cvt_parts[b["w"]]) if pc0 < b["c0"] + b["ncols"] and b["c0"] < pc0 + pcn]
                dma("sp", wsl[:, g % NSLOT, 0:n].rearrange("p (kc n) -> p kc n", kc=b["kcn"]), src, rk, [("wsl", g % NSLOT)], name="wld")
                wstate["next_load"] += 1
            assert gi > wstate["next_load"] - 1 - NSLOT, (name, gi, wstate["next_load"])
            b = WBLKS[gi % NWB]
            n = b["kcn"] * b["ncols"]
            return wsl[:, gi % NSLOT, 0:n].rearrange("p (kc n) -> p kc n", kc=b["kcn"]), ("wsl", gi % NSLOT)

        ntiles_total = NT + (1 if with_sample else 0)

        dma("sp", cst_f[:], cst_in[:, :, :], [], ["cst"])
        dma("sp", gains[:], gains_in[:, :], [], ["gains"])
        dma("sp", d_n[:], dn_in[:, :], [], ["d_n"])
        dma("sp", ssmn[:], ssmn_in[:, :, :], [], ["s:ssmn"])
        dma("sp", s0_sb[:], s0[:, :, :, :], [], ["s:s0"])
        copy("dve", cb_b[:, 0:3, :], cst_f[:, 0:3, :], ["cst"], ["cstb"])
        ts("dve", mb_b, cst_f[:, 1, :], 30000.0, ALU.mult, ["cst"], ["cstb"])
        memset("dve", qz[:].rearrange("p c h t -> p (c h t)"), 0.0, ["qz"])
        memset("dve", ones_f[:], 1.0, ["ones_f"])
        memset("dve", ones_b[:], 1.0, ["ones_b"])
        memset("dve", zeros_b[:], 0.0, ["zeros_b"])
        memset("dve", carry[:], 0.0, ["s:carry"])

        checkpoint("s0")
        cre_t = fout[:].rearrange("p a t -> p (a t)")
        cim_t = xT[:].rearrange("p a t -> p (a t)")
        FO = [("fout", c) for c in range(8)]
        XT = [("xT", c) for c in range(8)]
        dma("sp", cre_t.rearrange("p (j f) -> p j f", j=16), cpad_in[:, 0, :, :], [], FO)
        dma("sp", cim_t.rearrange("p (j f) -> p j f", j=16), cpad_in[:, 1, :, :], [], XT)
        copy("dve", Cp[:, :, 0, :], cre_t.rearrange("p (j f) -> p j f", j=16), FO, ["Cp"])
        ts("dve", Cp[:, :, 1, :], cim_t.rearrange("p (j f) -> p j f", j=16), -1.0, ALU.mult, XT, ["Cp"])

        checkpoint("s1")
        A_RE, A_IM, LDT = ssmn[:, 0, :], ssmn[:, 1, :], ssmn[:, 2, :]
        names = ["dt", "ar", "ai", "rho", "y", "kf", "fr", "s1", "s2", "c1", "sin", "cos", "lbr", "lbi", "nr",
                 "den", "rden", "fre", "fim", "t0", "t1", "t2", "t3", "t4"]
        V = {n: sm[:, i, :] for i, n in enumerate(names)}

        def K(n):
            return "s:" + n

        act(V["dt"], LDT, AF.Exp, ["s:ssmn"], [K("dt")])
        tt("dve", V["ar"], A_RE, V["dt"], ALU.mult, ["s:ssmn", K("dt")], [K("ar")])
        tt("dve", V["ai"], A_IM, V["dt"], ALU.mult, ["s:ssmn", K("dt")], [K("ai")])
        act(V["rho"], V["ar"], AF.Exp, [K("ar")], [K("rho")])
        ts("dve", V["y"], V["ai"], 1.0 / (2.0 * math.pi), ALU.mult, [K("ai")], [K("y")])
        copy("dve", smi[:], V["y"], [K("y")], [K("ki")])
        copy("dve", V["kf"], smi[:], [K("ki")], [K("kf")])
        tt("dve", V["fr"], V["y"], V["kf"], ALU.subtract, [K("y"), K("kf")], [K("fr")])
        act(V["s1"], V["fr"], AF.Sin, [K("fr")], [K("s1")], scale=math.pi)
        act(V["s2"], V["fr"], AF.Sin, [K("fr")], [K("s2")], scale=math.pi / 2.0)
        tt("dve", V["t0"], V["s2"], V["s2"], ALU.mult, [K("s2")], [K("t0")])
        ts("dve", V["c1"], V["t0"], -2.0, ALU.mult, [K("t0")], [K("c1")], s2=1.0, op1=ALU.add)
        tt("dve", V["t1"], V["s1"], V["c1"], ALU.mult, [K("s1"), K("c1")], [K("t1")])
        ts("dve", V["sin"], V["t1"], 2.0, ALU.mult, [K("t1")], [K("sin")])
        tt("dve", V["t2"], V["s1"], V["s1"], ALU.mult, [K("s1")], [K("t2")])
        ts("dve", V["cos"], V["t2"], -2.0, ALU.mult, [K("t2")], [K("cos")], s2=1.0, op1=ALU.add)
        tt("dve", V["lbr"], V["rho"], V["cos"], ALU.mult, [K("rho"), K("cos")], [K("lbr")])
        tt("dve", V["lbi"], V["rho"], V["sin"], ALU.mult, [K("rho"), K("sin")], [K("lbi")])
        ts("dve", V["nr"], V["lbr"], -1.0, ALU.add, [K("lbr")], [K("nr")])
        tt("dve", V["t3"], A_RE, A_RE, ALU.mult, ["s:ssmn"], [K("t3")])
        tt("dve", V["t4"], A_IM, A_IM, ALU.mult, ["s:ssmn"], [K("t4")])
        tt("dve", V["den"], V["t3"], V["t4"], ALU.add, [K("t3"), K("t4")], [K("den")])
        S.add("dve", lambda e: e.reciprocal(out=V["rden"], in_=V["den"]), reads=[K("den")], writes=[K("rden")])
        tt("dve", V["t0"], V["nr"], A_RE, ALU.mult, [K("nr"), "s:ssmn"], [K("t0")])
        tt("dve", V["t1"], V["lbi"], A_IM, ALU.mult, [K("lbi"), "s:ssmn"], [K("t1")])
        tt("dve", V["t2"], V["t0"], V["t1"], ALU.add, [K("t0"), K("t1")], [K("t2")])
        tt("dve", V["fre"], V["t2"], V["rden"], ALU.mult, [K("t2"), K("rden")], [K("fre")])
        tt("dve", V["t3"], V["lbi"], A_RE, ALU.mult, [K("lbi"), "s:ssmn"], [K("t3")])
        tt("dve", V["t4"], V["nr"], A_IM, ALU.mult, [K("nr"), "s:ssmn"], [K("t4")])
        tt("dve", V["t0"], V["t3"], V["t4"], ALU.subtract, [K("t3"), K("t4")], [K("t0")])
        tt("dve", V["fim"], V["t0"], V["rden"], ALU.mult, [K("t0"), K("rden")], [K("fim")])

        checkpoint("s2")
        RK_ALL = [("R", i) for i in range(NR)] + FO + XT
        tabT = R[:, 0:16, :].rearrange("p a t -> p (a t)").rearrange("p (c t j) -> p c t j", c=2, t=128)
        tA = fout[:].rearrange("p a t -> p (a t)")[:, 0:1024].rearrange("p (t j) -> p t j", j=16)
        tB = xT[:].rearrange("p a t -> p (a t)")[:, 0:1024].rearrange("p (t j) -> p t j", j=16)
        S.add("dve", lambda e: e.memset(R[:, 16, :], 0.0), reads=[], writes=RK_ALL)
        copy("dve", tabT[:, 0, 0, :], V["cos"], [K("cos"), ("R", 16)], ["s:tab"])
        copy("dve", tabT[:, 1, 0, :], V["sin"], [K("sin")], ["s:tab"])
        k = 1
        while k < 128:
            cr = tabT[:, 0, k - 1:k, :].to_broadcast([128, k, 16])
            ci = tabT[:, 1, k - 1:k, :].to_broadcast([128, k, 16])
            are = tabT[:, 0, 0:k, :]
            aim = tabT[:, 1, 0:k, :]
            tt("dve", tA[:, 0:k, :], are, cr, ALU.mult, ["s:tab"], ["s:tA"])
            tt("dve", tB[:, 0:k, :], aim, ci, ALU.mult, ["s:tab"], ["s:tB"])
            tt("dve", tabT[:, 0, k:2 * k, :], tA[:, 0:k, :], tB[:, 0:k, :], ALU.subtract, ["s:tA", "s:tB"], ["s:tab2"])
            tt("dve", tA[:, 0:k, :], are, ci, ALU.mult, ["s:tab", "s:tab2"], ["s:tA"])
            tt("dve", tB[:, 0:k, :], aim, cr, ALU.mult, ["s:tab", "s:tab2"], ["s:tB"])
            tt("dve", tabT[:, 1, k:2 * k, :], tA[:, 0:k, :], tB[:, 0:k, :], ALU.add, ["s:tA", "s:tB"], ["s:tab"])
            k *= 2
        for j in range(16):
            for comp in range(2):
                copy("dve" if comp == 0 else "act", cs[:, j, comp, :], tabT[:, comp, :, j], ["s:tab", "s:tab2"], ["s:cs"])
            ts("dve", rho_b[:, j, :], ones_f[:, 0:128], V["rho"][:, j:j + 1], ALU.mult, ["ones_f", K("rho")], ["rho_b"])
        ts("dve", csn[:, 0, :], cs[:, :, 1, 63], -1.0, ALU.mult, ["s:cs"], ["s:csn"])
        ts("dve", csn[:, 1, :], cs[:, :, 1, 127], -1.0, ALU.mult, ["s:cs"], ["s:csn"])
        S.add("dve", lambda e: e.memset(R[:, 16, :], 0.0), reads=["s:cs"], writes=RK_ALL)

        checkpoint("s3")
        Fre = R[:, 0:8, :].rearrange("p a t -> p (a t)")
        Fim = R[:, 8:16, :].rearrange("p a t -> p (a t)")
        diag = sig[:, 0:2, :].rearrange("p a t -> p (a t)")
        for comp, (fv, Fdst) in enumerate(((V["fre"], Fre), (V["fim"], Fim))):
            for q4 in range(4):
                bank = pbank[q4 % 4]
                for jj in range(4):
                    j = q4 * 4 + jj
                    dsl = diag[:, (jj % 4) * 128:(jj % 4 + 1) * 128]
                    ts("dve", dsl, ident_f, fv[:, j:j + 1], ALU.mult, ["cst", K("fre"), K("fim")], [("sig", jj // 2)])
                    mm_group(bank[:, jj * 128:(jj + 1) * 128], [(ones_f[:], dsl)], [("sig", jj // 2), "ones_f"], [("PB", q4 % 4)])
                copy("dve", Fdst[:, q4 * 512:(q4 + 1) * 512], bank[:, :], [("PB", q4 % 4)], [("R", 8 * comp + 2 * q4), ("R", 8 * comp + 2 * q4 + 1)])
        checkpoint("s4")
        Bre = xin[:].rearrange("p a f -> p (a f)").rearrange("p (m n) -> p m n", m=4)
        Bim = kvtok[:].rearrange("p a f -> p (a f)").rearrange("p (m n) -> p m n", m=4)
        dma("sp", Bre, bpad_in[:, 0, :, :], [], ["xin"])
        dma("sp", Bim, bpad_in[:, 1, :, :], [], ["kvtok"])
        Fre4 = Fre.rearrange("p (m n) -> p m n", m=4)
        Fim4 = Fim.rearrange("p (m n) -> p m n", m=4)
        t1v = fout[:].rearrange("p a t -> p (a t)").rearrange("p (m n) -> p m n", m=4)
        t2v = xT[:].rearrange("p a t -> p (a t)").rearrange("p (m n) -> p m n", m=4)
        RK8a = [("R", i) for i in range(8)]
        RK8b = [("R", i) for i in range(8, 16)]
        tt("dve", t1v, Fre4, Bre, ALU.mult, RK8a + ["xin", "Cp"], FO)
        tt("dve", t2v, Fim4, Bim, ALU.mult, RK8b + ["kvtok", "Cp"], XT)
        tt("dve", bbT[:, :, 0, :], t1v, t2v, ALU.subtract, FO + XT, ["bbT"])
        tt("dve", t1v, Fre4, Bim, ALU.mult, RK8a + ["kvtok"], FO)
        tt("dve", t2v, Fim4, Bre, ALU.mult, RK8b + ["xin"], XT)
        tt("dve", bbT[:, :, 1, :], t1v, t2v, ALU.add, FO + XT, ["bbT"])
        tap("bbT", bbT[:].rearrange("p a b c -> p (a b c)"), [128, 4096], ["bbT"]) if False else None
        checkpoint("s5")
        copy("dve", carry_s[:, :, :, 0], s0_sb[:, :, 0, :], ["s:s0"], ["s:carry_s"])
        copy("dve", carry_s[:, :, :, 1], s0_sb[:, :, 1, :], ["s:s0"], ["s:carry_s"])

        def rmsnorm(N, src3, src_keys, gidx, mode, tagq):
            MGK = [("mg", c) for c in range(8)]
            act(mg[:, :, 0:N], src3, AF.Square, list(src_keys), MGK)
            pr, pk = galloc()
            mm_group(pr[:, 0:N], [(ones_b[:], mg[:, c, 0:N]) for c in range(8)], MGK + ["ones_b"], [pk])
            act(sig[:, 2, 0:N], pr[:, 0:N], AF.Ln, [pk], [("sig", 2)], scale=1.0 / D, bias=EPS)
            act(rstd[:, 0:N], sig[:, 2, 0:N], AF.Exp, [("sig", 2)], ["rstd"], scale=-0.5)
            if mode == "bf":
                for c in range(8):
                    g = gains[:, gidx * 8 + c:gidx * 8 + c + 1]
                    stt(nT[:, c, 0:N], src3[:, c, :], g, rstd[:, 0:N], ALU.mult, ALU.mult, [src_keys[c], "rstd", "gains"], [("nT", c)])
            else:
                tt("dve", src3, src3, rstd[:, 0:N].unsqueeze(1).to_broadcast([128, 8, N]), ALU.mult, list(src_keys) + ["rstd"], list(src_keys))
                for c in range(8):
                    g = gains[:, gidx * 8 + c:gidx * 8 + c + 1]
                    stt(xT[:, c, 0:N], src3[:, c, :], g, xT[:, c, 0:N], ALU.mult, ALU.add, [src_keys[c], ("xT", c), "gains"], [("xT", c)])

        NTK = [("nT", c) for c in range(8)]

        def fm_proj(ti, N, wname, rhs_buf, rhs_keys, nkc, ncol_chunks, consume):
            wv, wk = wuse(ti, wname)
            for jj in range(ncol_chunks):
                pr, pk = galloc()
                mm_group(pr[:, 0:N], [(wv[:, kc, jj * 128:(jj + 1) * 128], rhs_buf[:, kc, 0:N]) for kc in range(nkc)],
                         [wk] + list(rhs_keys), [pk], name=wname)
                consume(jj, pr[:, 0:N], pk)

        def ssm_gen(ti, N, segs, carry_of, ybank=4, carry_on_act=False):
            T = segs[0][1]
            nseg = len(segs)
            Ybank = pbank[ybank]
            YK = ("PB", ybank)

            def v3d(ap):
                return ap.rearrange("p (s t) -> p s t", t=T)

            for jp in range(8):
                ctxs = []
                for u_ in range(2):
                    j = 2 * jp + u_
                    m, jj = j // 4, j % 4
                    c = dict(j=j, m=m, jj=jj,
                             cosT=cs[:, j, 0, 0:T].unsqueeze(1).to_broadcast([128, nseg, T]),
                             sinT=cs[:, j, 1, 0:T].unsqueeze(1).to_broadcast([128, nseg, T]))
                    if u_ == 0:
                        c.update(xre=Rf(4, N), xim=Rf(5, N), kx0=("R", 4), kx1=("R", 5), qre=Rf(6, N), qim=Rf(7, N), kq0=("R", 6), kq1=("R", 7),
                                 ct=ctmp[:, 0:2], kct="s:ctA", sl=8)
                    else:
                        c.update(xre=sig[:, 2, 0:N], xim=rstd[:, 0:N], kx0=("sig", 2), kx1="rstd", qre=Rf(23, N), qim=Rf(24, N),
                                 kq0=("R", 23), kq1=("R", 24), ct=ctmp[:, 2:4], kct="s:ctB", sl=9)
                    ctxs.append(c)

                def stage_x(c):
                    m, jj = c["m"], c["jj"]
                    pa, pka = galloc()
                    pb, pkb = galloc()
                    mm_group(pa[:, 0:N], [(bbT[:, m, 0, jj * 128:(jj + 1) * 128], uT[:, m, 0:N])], ["bbT", ("uT", m)], [pka], name="bu_re")
                    mm_group(pb[:, 0:N], [(bbT[:, m, 1, jj * 128:(jj + 1) * 128], uT[:, m, 0:N])], ["bbT", ("uT", m)], [pkb], name="bu_im")
                    c.update(pka=pka, pkb=pkb, bre=v3d(pa[:, 0:N]), bim=v3d(pb[:, 0:N]))
                    tt("dve", v3d(Rf(0, N)), c["bre"], c["cosT"], ALU.mult, [c["pka"], "s:cs"], [("R", 0)])
                    tt("dve", v3d(Rf(2, N)), c["bim"], c["cosT"], ALU.mult, [c["pkb"], "s:cs"], [("R", 2)])
                    tt("dve", v3d(Rf(1, N)), c["bim"], c["sinT"], ALU.mult, [c["pkb"], "s:cs"], [("R", 1)])
                    tt("dve", v3d(Rf(3, N)), c["bre"], c["sinT"], ALU.mult, [c["pka"], "s:cs"], [("R", 3)])
                    tt("dve", c["xre"], Rf(0, N), Rf(1, N), ALU.add, [("R", 0), ("R", 1)], [c["kx0"]])
                    tt("dve", c["xim"], Rf(2, N), Rf(3, N), ALU.subtract, [("R", 2), ("R", 3)], [c["kx1"]])

                def stage_scan(c, si):
                    c0, Ts = segs[si]
                    j = c["j"]
                    cre, cim = carry_of(si)(j)
                    ckey = ("s:carry", si, j)
                    S.add("dve", lambda e, c0=c0, Ts=Ts, cre=cre, j=j, qre=c["qre"], xre=c["xre"]: e.tensor_tensor_scan(
                        out=qre[:, c0:c0 + Ts], data0=rho_b[:, j, 0:Ts], data1=xre[:, c0:c0 + Ts], initial=cre,
                        op0=ALU.mult, op1=ALU.add), reads=[c["kx0"], "rho_b", "s:carry", "s:carry_s", ckey], writes=[c["kq0"]])
                    S.add("dve", lambda e, c0=c0, Ts=Ts, cim=cim, j=j, qim=c["qim"], xim=c["xim"]: e.tensor_tensor_scan(
                        out=qim[:, c0:c0 + Ts], data0=rho_b[:, j, 0:Ts], data1=xim[:, c0:c0 + Ts], initial=cim,
                        op0=ALU.mult, op1=ALU.add), reads=[c["kx1"], "rho_b", "s:carry", "s:carry_s", ckey], writes=[c["kq1"]])

                def stage_carry(c, si):
                    c0, Ts = segs[si]
                    j = c["j"]
                    cre, cim = carry_of(si)(j)
                    ckey = ("s:carry", si, j)
                    cT = cs[:, j, 0, Ts - 1:Ts]
                    sT = cs[:, j, 1, Ts - 1:Ts]
                    ql_re = c["qre"][:, c0 + Ts - 1:c0 + Ts]
                    ql_im = c["qim"][:, c0 + Ts - 1:c0 + Ts]
                    ct = c["ct"]
                    if carry_on_act:
                        nsT = csn[:, 0 if Ts == 64 else 1, j:j + 1]
                        act(ct[:, 0:1], ql_im, AF.Copy, [c["kq0"], c["kq1"], "s:csn"], [c["kct"] + "0"], scale=nsT)
                        act(ct[:, 1:2], ql_re, AF.Copy, [c["kq0"], c["kq1"], "s:cs"], [c["kct"] + "1"], scale=sT)
                        act(cre, ql_re, AF.Identity, [c["kq0"], c["kq1"], c["kct"] + "0", "s:cs"], [ckey], scale=cT, bias=ct[:, 0:1])
                        act(cim, ql_im, AF.Identity, [c["kq0"], c["kq1"], c["kct"] + "1", "s:cs"], [ckey], scale=cT, bias=ct[:, 1:2])
                    else:
                        ts("dve", ct[:, 0:1], ql_im, sT, ALU.mult, [c["kq0"], c["kq1"], "s:cs"], [c["kct"] + "0"])
                        ts("dve", ct[:, 1:2], ql_re, sT, ALU.mult, [c["kq0"], c["kq1"], "s:cs"], [c["kct"] + "1"])
                        stt(cre, ql_re, cT, ct[:, 0:1], ALU.mult, ALU.subtract, [c["kq0"], c["kq1"], c["kct"] + "0", "s:cs"], [ckey])
                        stt(cim, ql_im, cT, ct[:, 1:2], ALU.mult, ALU.add, [c["kq0"], c["kq1"], c["kct"] + "1", "s:cs"], [ckey])

                def stage_pool(c):
                    j, m, jj, sl = c["j"], c["m"], c["jj"], c["sl"]
                    sbf = Rb(sl)[:, 0:2 * N].rearrange("p (a t) -> p a t", a=2)
                    qre, qim = c["qre"], c["qim"]
                    tt("pool", v3d(Rf(21, N)), v3d(qre), c["cosT"], ALU.mult, [c["kq0"], "s:cs"], [("R", 21)])
                    tt("pool", v3d(Rf(22, N)), v3d(qim), c["sinT"], ALU.mult, [c["kq1"], "s:cs"], [("R", 22)])
                    tt("pool", sbf[:, 0, :], Rf(21, N), Rf(22, N), ALU.subtract, [("R", 21), ("R", 22)], [("R", sl)])
                    tt("pool", v3d(Rf(21, N)), v3d(qim), c["cosT"], ALU.mult, [c["kq1"], "s:cs"], [("R", 21)])
                    tt("pool", v3d(Rf(22, N)), v3d(qre), c["sinT"], ALU.mult, [c["kq0"], "s:cs"], [("R", 22)])
                    tt("pool", sbf[:, 1, :], Rf(21, N), Rf(22, N), ALU.add, [("R", 21), ("R", 22)], [("R", sl)])
                    mm_group(Ybank[:, 0:N], [(Cp[:, j, 0, :], sbf[:, 0, :]), (Cp[:, j, 1, :], sbf[:, 1, :])], [("R", sl), "Cp"], [YK],
                             start=(jj == 0), stop=(jj == 3), name="Cproj")

                A, B = ctxs
                stage_x(A); yield
                stage_x(B); yield
                for si in range(nseg):
                    stage_scan(A, si); yield
                    stage_scan(B, si); yield
                    stage_carry(A, si); yield
                    stage_carry(B, si); yield
                stage_pool(A); yield
                stage_pool(B); yield
                if B["jj"] == 3:
                    m = B["m"]
                    y = Rf(10, N)
                    tt("dve", y, Ybank[:, 0:N], du[:, m, 0:N], ALU.add, [YK, ("du", m)], [("R", 10)])
                    ge = "pool" if carry_on_act else "dve"
                    tt(ge, Rf(11, N), y, y, ALU.mult, [("R", 10)], [("R", 11)])
                    yield
                    ts(ge, Rf(11, N), Rf(11, N), 0.044715, ALU.mult, [("R", 11)], [("R", 11)], s2=1.0, op1=ALU.add)
                    yield
                    tt(ge, Rf(11, N), Rf(11, N), y, ALU.mult, [("R", 11), ("R", 10)], [("R", 11)])
                    act(sig[:, 0, 0:N], Rf(11, N), AF.Sigmoid, [("R", 11)], [("sig", 0)], scale=1.5957691216057308)
                    yield
                    tt(ge, Rf(12 + m, N), y, sig[:, 0, 0:N], ALU.mult, [("R", 10), ("sig", 0)], [("R", 12 + m)])
                    copy("pool", mg[:, m, 0:N], Rf(12 + m, N), [("R", 12 + m)], [("mg", m)])
                    yield

        def ssm_glu(ti, N):
            def cons(jo, pr, pk):
                act(sig[:, jo % 2, 0:N], pr, AF.Sigmoid, [pk], [("sig", jo % 2)])
                tt("dve", o_ssm[:, jo, 0:N], Rf(12 + jo, N), sig[:, jo % 2, 0:N], ALU.mult, [("R", 12 + jo), ("sig", jo % 2)], [("o_ssm", jo)])
            fm_proj(ti, N, "glu", mg, [("mg", c) for c in range(4)], 4, 4, cons)

        def attn_prompt(ti, N, pump):
            t0 = ti * TT
            kmax = (t0 + N) // 128 - 1
            kbs = list(range(kmax, -1, -1))
            nkb = len(kbs)
            nones_f = cst_f[:, 2, :]
            for h in range(NH):
                c, half, hp = h // 2, h % 2, 64 * (h % 2)
                Ob = pbank[5 + (h % 2)]
                Okey = ("PB", 5 + (h % 2))
                Oap = Ob[hp:hp + 64, 0:N]
                mm_group(Oap, [(zeros_b[:, 0:64], qz[:, 0, 0, 0:N])], ["zeros_b", ("qz", 0)], [Okey], start=True, stop=False, name="Ozero")
                memset("pool", Rf(19, N), 0.0, [("R", 19)])

                def c0_of(kb):
                    return max(0, kb * 128 - t0)

                def lp_of(par):
                    return Rb(18)[:, par * 256:(par + 1) * 256]

                def w_of(par):
                    return Rb(20)[:, par * 256:(par + 1) * 256]

                def zmm(e, out, kb, c0, first, last, c=c, half=half):
                    diag = kb * 128 >= t0
                    ins = e.matmul(out[:, c0:N], lhsT=kT3[:, c, kb * 128:(kb + 1) * 128], rhs=qz[:, c, half, c0:N],
                                   start=first, stop=(last and not diag))
                    if diag:
                        ins = e.matmul(out[:, c0:c0 + 128], lhsT=ident_b, rhs=mb_b, start=False, stop=last)
                    return ins

                def stage1a(i):
                    kb = kbs[i]
                    c0 = c0_of(kb)
                    par = i % 2
                    pz, pzk = galloc()
                    S.add("pe", lambda e, pz=pz, kb=kb, c0=c0, zmm=zmm: zmm(e, pz, kb, c0, True, True),
                          reads=[("kT", kb // 2), ("qz", c), "cstb"], writes=[pzk], name="QK")
                    act(Rf(16 + par, N)[:, c0:N], pz[:, c0:N], AF.Exp, [pzk], [("R", 16 + par)])

                def stage1b(i):
                    kb = kbs[i]
                    c0 = c0_of(kb)
                    par = i % 2
                    act(lp_of(par)[:, c0:N], Rf(16 + par, N)[:, c0:N], AF.Ln, [("R", 16 + par)], [("R", 18, par)], bias=1.0)

                def stage2(i):
                    kb = kbs[i]
                    c0 = c0_of(kb)
                    par = i % 2
                    lp = lp_of(par)
                    psr, psk = galloc()

                    def sf(e, i=i, kb=kb, c0=c0, lp=lp, psr=psr, zmm=zmm):
                        zmm(e, psr, kb, c0, True, False)
                        ins = e.matmul(psr[:, c0:N], lhsT=tri_b, rhs=lp[:, c0:N], start=False, stop=(i == 0))
                        if i > 0:
                            ins = e.matmul(psr[:, c0:N], lhsT=nones_f, rhs=Rf(19, N)[:, c0:N], start=False, stop=True)
                        return ins
                    S.add("pe", sf, reads=[("R", 18, par), "cstb", "cst", ("kT", kb // 2), ("qz", c)] + ([("R", 19)] if i > 0 else []),
                          writes=[psk], name="S")
                    act(w_of(par)[:, c0:N], psr[:, c0:N], AF.Exp, [psk], [("R", 20, par)])
                    if i + 1 < nkb:
                        tt("pool", Rf(19, N)[:, c0:N], Rf(19, N)[:, c0:N], lp[:, c0:N], ALU.add, [("R", 19), ("R", 18, par)], [("R", 19)])

                def stage3(i):
                    kb = kbs[i]
                    c0 = c0_of(kb)
                    par = i % 2
                    w = w_of(par)
                    mm_group(Ob[hp:hp + 64, c0:N], [(v3[:, kb, h * 64:(h + 1) * 64], w[:, c0:N])], [("v", kb), ("R", 20, par)], [Okey],
                             start=False, stop=(i == nkb - 1), name="WV")

                for step in range(nkb + 2):
                    if step < nkb:
                        stage1a(step)
                    if 0 <= step - 1 < nkb:
                        stage2(step - 1)
                    if step < nkb:
                        stage1b(step)
                    if 0 <= step - 2 < nkb:
                        stage3(step - 2)
                    pump()
                copy("act", o_att[hp:hp + 64, c, 0:N], Oap, [Okey], [("o_att", c)])

        def attn_sample(ti, pump):
            old = [("kT", i) for i in range(8)] + [("v", i) for i in range(16)]
            pend = S.pending(old)
            for e in ("pe", "act", "dve", "pool"):
                S.barrier_wait(e, pend, name="kvfence")
            NST = 3
            kblk = [kT_all[:, i * 512:(i + 1) * 512] for i in range(NST)]
            kTb = [kT_all[:, 2048 + i * 512:2048 + (i + 1) * 512].rearrange("p (c t) -> p c t", c=4) for i in range(2)]
            NSTV = 5
            vblk = [v_all[:, 2048 + i * 512:2048 + (i + 1) * 512] for i in range(NSTV)]
            vnew = v_all[:, 0:1024].rearrange("p (s f) -> p s f", s=2)
            Oacc = v_all[:, 6656:7168].bitcast(F32)
            reserved.update((4, 5, 6, 7))
            UK = [("qz", c) for c in range(4)]
            for s in range(2):
                e32 = [kT_all[:, 3072:4096].bitcast(F32), kT_all[:, 4096:5120].bitcast(F32)]
                exs = [kT_all[:, 5120:6144].bitcast(F32), kT_all[:, 6144:7168].bitcast(F32)]
                lpb = [kT_all[:, 7168:7680], kT_all[:, 7680:8192]]
                ls32 = v_all[:, 4608:5632].bitcast(F32)
                wb = [v_all[:, 5632:6144], v_all[:, 6144:6656]]
                ke32 = [[("sa", 0)], [("sa", 1)]]
                kls = [("sa", 6)]
                kex = [[("sa", 2)], [("sa", 3)]]
                memset("pool", ls32, 0.0, kls)
                blocks = ["new"] + list(range(PAST // 128 - 1, -1, -1))
                nb = len(blocks)
                zb = [pbank[4], pbank[5]]
                sbk = [pbank[6], pbank[7]]

                def load(i):
                    b = blocks[i]
                    if b == "new":
                        return
                    sl = i % NST
                    dma("pool", kblk[sl], ck[s, b * 128:(b + 1) * 128, :], [], [("kblk", sl)], name="kld")
                    dma("pool", vblk[i % NSTV], cv[s, b * 128:(b + 1) * 128, :], [], [("vblk", i % NSTV)], name="vld")

                def stage1(i):
                    b = blocks[i]
                    par = i % 2
                    Z = zb[par]
                    zk = ("PB", 4 + par)
                    if b == "new":
                        KP = 64
                        prs = []
                        for c in range(4):
                            prs.append((kTn[:, c, s * 64:(s + 1) * 64], qz[:, c, :, s * 64:(s + 1) * 64]))
                        rd = ["kTn"] + UK
                    else:
                        KP = 128
                        sl = i % NST
                        pr, pk = galloc()
                        prb = pr.bitcast(BF16)
                        def trf(e, sl=sl, prb=prb):
                            ins = None
                            for c in range(4):
                                ins = e.transpose(out=prb[:, c * 128:(c + 1) * 128], in_=kblk[sl][:, c * 128:(c + 1) * 128], identity=ident_b)
                            return ins
                        S.add("pe", trf, reads=[("kblk", sl), "cstb"], writes=[pk], name="ktr")
                        copy("dve", kTb[par][:].rearrange("p c t -> p (c t)"), prb, [pk], [("kTb", par)])
                        prs = []
                        for c in range(4):
                            prs.append((kTb[par][:, c, :], qz[:, c, :, s * 64:(s + 1) * 64]))
                        rd = [("kTb", par)] + UK

                    def zf(e, prs=prs, Z=Z, KP=KP):
                        ins = None
                        for c, (l, r) in enumerate(prs):
                            ins = e.matmul(Z[0:KP, c * 128:(c + 1) * 128].rearrange("p (h t) -> p h t", h=2), lhsT=l, rhs=r, start=True, stop=True)
                        return ins
                    S.add("pe", zf, reads=rd, writes=[zk], name="QKs")
                    act(e32[par][0:KP, :], Z[0:KP, :], AF.Exp, [zk], ke32[par])
                    if b == "new":
                        ev = e32[par][0:64, :].rearrange("p (h t) -> p h t", h=NH)
                        tt("dve", ev, ev, tmask_f[0:64, 0:64].unsqueeze(1).to_broadcast([64, NH, 64]), ALU.mult, ke32[par] + ["cst"], ke32[par])
                    act(lpb[par][0:KP, :], e32[par][0:KP, :], AF.Ln, ke32[par], [("sa", 4 + par)], bias=1.0)

                def stage2(i):
                    b = blocks[i]
                    par = i % 2
                    KP = 64 if b == "new" else 128
                    SP = sbk[par]
                    sk = ("PB", 6 + par)
                    pairs = [(tri_b[0:KP, 0:KP], lpb[par][0:KP, :])]
                    rd = [("sa", 4 + par), "cstb"]
                    if i > 0:
                        pairs.append((cst_f[:, 2, 0:KP], ls32[:, :]))
                        rd += kls + ["cst"]
                    mm_group(SP[0:KP, :], pairs, rd, [sk], name="Ss")
                    if i + 1 < nb:
                        tt("dve", ls32[0:KP, :], ls32[0:KP, :], lpb[par][0:KP, :], ALU.add, kls + [("sa", 4 + par)], kls)
                    act(exs[par][0:KP, :], SP[0:KP, :], AF.Exp, [sk], kex[par])
                    tt("dve", wb[par][0:KP, :], e32[par][0:KP, :], exs[par][0:KP, :], ALU.mult, ke32[par] + kex[par], [("sa", 7 + par)])

                def stage3(i):
                    b = blocks[i]
                    par = i % 2
                    KP = 64 if b == "new" else 128
                    pr, pk = galloc()
                    prs = []
                    for h in range(NH):
                        c, hp = h // 2, 64 * (h % 2)
                        if b == "new":
                            vsrc = vnew[0:64, s, h * 64:(h + 1) * 64]
                        else:
                            vsrc = vblk[i % NSTV][:, h * 64:(h + 1) * 64]
                        prs.append((pr[hp:hp + 64, c * 64:(c + 1) * 64], vsrc, wb[par][0:KP, h * 64:(h + 1) * 64]))

                    def wvf(e, prs=prs):
                        ins = None
                        for (o, l, r) in prs:
                            ins = e.matmul(o, lhsT=l, rhs=r, start=True, stop=True)
                        return ins
                    rd = [("sa", 7 + par)] + (["vnew"] if b == "new" else [("vblk", i % NSTV)])
                    S.add("pe", wvf, reads=rd, writes=[pk], name="WVs")
                    if i == 0:
                        copy("dve", Oacc, pr[:, :], [pk], [("sa", 9)])
                    else:
                        tt("dve", Oacc, pr[:, :], Oacc, ALU.add, [pk, ("sa", 9)], [("sa", 9)])

                load(0); load(1)
                for step in range(nb + 2):
                    checkpoint(f"SE_{s}_{step}")
                    if step + 2 < nb:
                        load(step + 2)
                    if step < nb:
                        stage1(step)
                    if 0 <= step - 1 < nb:
                        stage2(step - 1)
                    if 0 <= step - 2 < nb:
                        stage3(step - 2)
                    pump()
                copy("dve", o_att[:, :, s * 64:(s + 1) * 64], Oacc.rearrange("p (c t) -> p c t", c=4), [("sa", 9)], [("o_att", c) for c in range(4)])

        def run_tile(ti, is_sample):
            N = 128 if is_sample else TT
            nb = N // 128
            t0 = ti * TT
            x_src = xs if is_sample else xp[t0:t0 + N, :]
            p_src = ps_in if is_sample else pp_in[t0:t0 + N, :]
            y_dst = ys if is_sample else yp[t0:t0 + N, :]

            for c in range(8):
                pr, pk = galloc()
                def trf(e, c=c, pr=pr):
                    ins = None
                    for b in range(nb):
                        ins = e.transpose(out=pr[:, b * 128:(b + 1) * 128], in_=xin[:, b, c * 128:(c + 1) * 128], identity=ident_f)
                    return ins
                S.add("pe", trf, reads=["xin", "cst"], writes=[pk], name="xtr")
                copy(evac_eng(), xT[:, c, 0:N], pr[:, 0:N], [pk], [("xT", c)])
            tap(f"xT{ti}", xT[:].rearrange("p a t -> p (a t)"), [128, 8 * TT], XT)

            checkpoint(("S" if is_sample else "") + "A")
            rmsnorm(N, xT[:, :, 0:N], XT, 0, "bf", "B")
            tap(f"nT{ti}", nT[:].rearrange("p a t -> p (a t)"), [128, 8 * TT], NTK) if False else None

            checkpoint(("S" if is_sample else "") + "B")
            for i in range(2):
                def cq(jj, pr, pk, i=i):
                    c = 2 * i + jj
                    act(qz[0:64, c, 0, 0:N], pr[0:64, :], AF.Copy, [pk], [("qz", c)], scale=0.125)
                    act(qz[64:128, c, 1, 0:N], pr[64:128, :], AF.Copy, [pk], [("qz", c)], scale=0.125)
                fm_proj(ti, N, f"q{i}", nT, NTK, 8, 2, cq)
            checkpoint(("S" if is_sample else "") + "C1")
            for i in range(2):
                def ckk(jj, pr, pk, i=i):
                    oc = 2 * i + jj
                    if is_sample:
                        copy(evac_eng(), kTn[:, oc, 0:N], pr, [pk], ["kTn"])
                    else:
                        copy(evac_eng(), kT3[:, oc, t0:t0 + N], pr, [pk], [("kT", ti)])
                fm_proj(ti, N, f"k{i}", nT, NTK, 8, 2, ckk)
                wv, wk = wuse(ti, f"k{i}")
                tok_proj(ti, N, is_sample, wv, wk, 256 * i, None)
            checkpoint(("S" if is_sample else "") + "C2")
            for i in range(2):
                wv, wk = wuse(ti, f"v{i}")
                tok_proj(ti, N, is_sample, wv, wk, 512 + 256 * i, i)
            checkpoint(("S" if is_sample else "") + "C3")
            for i in range(2):
                def cu(jj, pr, pk, i=i):
                    oc = 2 * i + jj
                    copy("dve", uT[:, oc, 0:N], pr, [pk], [("uT", oc), pk])
                    act(du[:, oc, 0:N], pr, AF.Copy, [pk, "d_n"], [("du", oc)], scale=d_n[:, oc:oc + 1])
                fm_proj(ti, N, f"u{i}", nT, NTK, 8, 2, cu)
            checkpoint(("S" if is_sample else "") + "C4")
            if not is_sample:
                if ti + 1 < NT:
                    xload(ti + 1, False)
                elif with_sample:
                    xload(NT, True)
            if is_sample:
                for s in range(2):
                    out_dmas.append(dma("pool", ks[s * 64:(s + 1) * 64, :], kvtok[0:64, s, 0:512], ["kvtok"], []))
                    out_dmas.append(dma("pool", vs[s * 64:(s + 1) * 64, :], kvtok[0:64, s, 512:1024], ["kvtok"], []))
            else:
                out_dmas.append(dma("pool", kp[t0:t0 + N, :].rearrange("(b p) f -> p b f", p=128), kvtok[:, 0:nb, 0:512], ["kvtok"], []))
                out_dmas.append(dma("pool", vp[t0:t0 + N, :].rearrange("(b p) f -> p b f", p=128), kvtok[:, 0:nb, 512:1024], ["kvtok"], []))

            checkpoint(("S" if is_sample else "") + "C")
            if is_sample:
                segs = [(0, 64), (64, 64)]
                def carry_of(si):
                    return lambda j: (carry_s[:, si, j, 0:1], carry_s[:, si, j, 1:2])
                reserved.clear(); reserved.update((3, 4, 5, 6, 7))
                gen = ssm_gen(ti, N, segs, carry_of, ybank=3)
                done = [False]

                def pump():
                    if done[0]:
                        return
                    for _ in range(2):
                        try:
                            next(gen)
                        except StopIteration:
                            done[0] = True
                            return
                attn_sample(ti, pump)
                for _ in gen:
                    pass
                reserved.clear()
                ssm_glu(ti, N)
            else:
                segs = [(0, 128), (128, 128)]
                def carry_of(si):
                    return lambda j: (carry[:, j, 0:1], carry[:, j, 1:2])
                reserved.clear(); reserved.update((4, 5, 6))
                gen = ssm_gen(ti, N, segs, carry_of, carry_on_act=(ti < 4))
                nsteps = NH * ((t0 + N) // 128 + 2)
                per = -(-150 // nsteps)
                done = [False]

                def pump():
                    if done[0]:
                        return
                    for _ in range(per):
                        try:
                            next(gen)
                        except StopIteration:
                            done[0] = True
                            return
                attn_prompt(ti, N, pump)
                for _ in gen:
                    pass
                reserved.clear()
                ssm_glu(ti, N)
            tap(f"ossm{ti}", o_ssm[:].rearrange("p a t -> p (a t)"), [128, 4 * TT], [("o_ssm", c) for c in range(4)])
            tap(f"oatt{ti}", o_att[:].rearrange("p a t -> p (a t)"), [128, 4 * TT], [("o_att", c) for c in range(4)])
            checkpoint(("S" if is_sample else "") + "E")
            dma("act", pst[:, 0:nb, :], p_src.rearrange("(b p) f -> p b f", p=128), [], [("du", 0), ("du", 1)], name="pld")
            OA = [("o_att", c) for c in range(4)]
            OS = [("o_ssm", c) for c in range(4)]
            for half in range(2):
                for pp2 in range(2):
                    p = 2 * half + pp2
                    gav, gak = wuse(ti, f"ga{p}")
                    gsv, gsk = wuse(ti, f"gs{p}")
                    bav, bak = wuse(ti, f"wba{half}")
                    bsv, bsk = wuse(ti, f"wbs{half}")
                    for jj in range(2):
                        oc = 2 * p + jj
                        lc = (oc % 4) * 128
                        pA, kA = galloc()
                        pB, kB = galloc()
                        pC, kC = galloc()
                        pD, kD = galloc()
                        mm_group(pA[:, 0:N], [(gav[:, kc, jj * 128:(jj + 1) * 128], nT[:, kc, 0:N]) for kc in range(8)], [gak] + NTK, [kA], name="ga")
                        mm_group(pB[:, 0:N], [(gsv[:, kc, jj * 128:(jj + 1) * 128], nT[:, kc, 0:N]) for kc in range(8)], [gsk] + NTK, [kB], name="gs")
                        mm_group(pC[:, 0:N], [(bav[:, kc, lc:lc + 128], o_att[:, kc, 0:N]) for kc in range(4)], [bak] + OA, [kC], name="ba")
                        mm_group(pD[:, 0:N], [(bsv[:, kc, lc:lc + 128], o_ssm[:, kc, 0:N]) for kc in range(4)], [bsk] + OS, [kD], name="bs")
                        act(sig[:, 0, 0:N], pA[:, 0:N], AF.Sigmoid, [kA], [("sig", 0)])
                        act(sig[:, 1, 0:N], pB[:, 0:N], AF.Sigmoid, [kB], [("sig", 1)])
                        tt("dve", sig[:, 0, 0:N], pC[:, 0:N], sig[:, 0, 0:N], ALU.mult, [kC, ("sig", 0)], [("sig", 0)])
                        tt("dve", sig[:, 1, 0:N], pD[:, 0:N], sig[:, 1, 0:N], ALU.mult, [kD, ("sig", 1)], [("sig", 1)])
                        tt("dve", mg[:, oc, 0:N], sig[:, 0, 0:N], sig[:, 1, 0:N], ALU.add, [("sig", 0), ("sig", 1)], [("mg", oc)])
            MG = [("mg", c) for c in range(8)]
            for i in range(4):
                def cwo(jj, pr, pk, i=i):
                    oc = 2 * i + jj
                    copy(evac_eng(), fout[:, oc, 0:N], pr, [pk], [("fout", oc)])
                fm_proj(ti, N, f"wo{i}", mg, MG, 8, 2, cwo)
            rmsnorm(N, fout[:, :, 0:N], FO, 1, "res", "F")
            tap(f"x1_{ti}", xT[:].rearrange("p a t -> p (a t)"), [128, 8 * TT], XT)

            checkpoint(("S" if is_sample else "") + "F")
            rmsnorm(N, xT[:, :, 0:N], XT, 2, "bf", "G")
            hid = R[:, 0:11, :].rearrange("p a t -> p (a t)").bitcast(BF16).rearrange("p (c t) -> p c t", c=22)

            def hk(ocf):
                return ("R", ocf // 2)
            for i in range(11):
                gv, gk = wuse(ti, f"fg{i}")
                uv, uk = wuse(ti, f"fu{i}")
                for jj in range(2):
                    ocf = 2 * i + jj
                    pA, kA = galloc()
                    pB, kB = galloc()
                    mm_group(pA[:, 0:N], [(gv[:, kc, jj * 128:(jj + 1) * 128], nT[:, kc, 0:N]) for kc in range(8)], [gk] + NTK, [kA], name="fg")
                    mm_group(pB[:, 0:N], [(uv[:, kc, jj * 128:(jj + 1) * 128], nT[:, kc, 0:N]) for kc in range(8)], [uk] + NTK, [kB], name="fu")
                    sl = ocf % 2
                    act(sig[:, sl, 0:N], pA[:, 0:N], AF.Silu, [kA], [("sig", sl)])
                    tt("dve", hid[:, ocf, 0:N], pB[:, 0:N], sig[:, sl, 0:N], ALU.mult, [kB, ("sig", sl)], [hk(ocf)])
            reserved.clear(); reserved.update((6, 7))
            for cb in range(4):
                for kg, (k0, kn) in enumerate(((0, 8), (8, 8), (16, 6))):
                    dv, dk = wuse(ti, f"fd{cb}_{kg}")
                    for jj in range(2):
                        bank = pbank[6 + jj]
                        mm_group(bank[:, 0:N], [(dv[:, kc, jj * 128:(jj + 1) * 128], hid[:, k0 + kc, 0:N]) for kc in range(kn)],
                                 [dk] + [("R", r) for r in range(11)], [("PB", 6 + jj)], start=(kg == 0), stop=(kg == 2), name="fd")
                for jj in range(2):
                    oc = 2 * cb + jj
                    copy(evac_eng(), fout[:, oc, 0:N], pbank[6 + jj][:, 0:N], [("PB", 6 + jj)], [("fout", oc)])
            reserved.clear()
            rmsnorm(N, fout[:, :, 0:N], FO, 3, "res", "G2")
            tap(f"x2_{ti}", xT[:].rearrange("p a t -> p (a t)"), [128, 8 * TT], XT)

            checkpoint(("S" if is_sample else "") + "G")
            rmsnorm(N, xT[:, :, 0:N], XT, 4, "bf", "H")
            for c2 in range(2):
                pr, pk = galloc()
                def trp(e, c2=c2, pr=pr):
                    ins = None
                    for b in range(nb):
                        ins = e.transpose(out=pr[:, b * 128:(b + 1) * 128], in_=pst[:, b, c2 * 128:(c2 + 1) * 128], identity=ident_f)
                    return ins
                S.add("pe", trp, reads=[("du", 0), ("du", 1), "cst"], writes=[pk], name="ptr")
                copy(evac_eng(), pT[:, c2, 0:N], pr[:, 0:N], [pk], [("pT", c2)])
            ppv = None
            for i in range(4):
                gv, gk = wuse(ti, f"pg{i}")
                if i == 0:
                    pass
                for jj in range(2):
                    oc = 2 * i + jj
                    pA, kA = galloc()
                    mm_group(pA[:, 0:N], [(gv[:, kc, jj * 128:(jj + 1) * 128], nT[:, kc, 0:N]) for kc in range(8)], [gk] + NTK, [kA], name="pg")
                    act(fout[:, oc, 0:N], pA[:, 0:N], AF.Sigmoid, [kA], [("fout", oc)])
            ppv, ppk = wuse(ti, "pp")
            for oc in range(8):
                pB, kB = galloc()
                mm_group(pB[:, 0:N], [(ppv[:, kc, oc * 128:(oc + 1) * 128], pT[:, kc, 0:N]) for kc in range(2)], [ppk, ("pT", 0), ("pT", 1)], [kB], name="pp")
                tt("dve", fout[:, oc, 0:N], pB[:, 0:N], fout[:, oc, 0:N], ALU.mult, [kB, ("fout", oc)], [("fout", oc)])
            rmsnorm(N, fout[:, :, 0:N], FO, 5, "res", "H2")

            checkpoint(("S" if is_sample else "") + "H")
            for b in range(nb):
                for half in range(2):
                    pr0, pk0 = galloc()
                    pr1, pk1 = galloc()
                    for (pr, pk, cbase) in ((pr0, pk0, 4 * half), (pr1, pk1, 4 * half + 2)):
                        def try_(e, pr=pr, cbase=cbase, b=b):
                            ins = None
                            for cc in range(2):
                                ins = e.transpose(out=pr[:, cc * 128:(cc + 1) * 128], in_=xT[:, cbase + cc, b * 128:(b + 1) * 128], identity=ident_f)
                            return ins
                        S.add("pe", try_, reads=[("xT", cbase), ("xT", cbase + 1), "cst"], writes=[pk], name="ytr")
                        copy(evac_eng(), kvtok[:, b, cbase * 128:(cbase + 2) * 128], pr[:, 0:256], [pk], ["kvtok"])
            out_dmas.append(dma("pool", y_dst.rearrange("(b p) f -> p b f", p=128), kvtok[:, 0:nb, :], ["kvtok"], [], name="yst"))

        def xload(ti, is_sample):
            N = 128 if is_sample else TT
            x_src = xs if is_sample else xp[ti * TT:ti * TT + N, :]
            dma("act", xin[:, 0:N // 128, :], x_src.rearrange("(b p) f -> p b f", p=128), [], ["xin"], name="xld")

        def tok_proj(ti, N, is_sample, wv, wk, col0, vidx):
            t0 = ti * TT
            if is_sample:
                for s in range(2):
                    pr, pk = galloc()
                    mm_group(pr[0:64, 0:256], [(nT[:, kc, s * 64:(s + 1) * 64], wv[:, kc, 0:256]) for kc in range(8)], [wk] + NTK, [pk], name="tokp")
                    copy("dve", kvtok[0:64, s, col0:col0 + 256], pr[0:64, 0:256], [pk], ["kvtok", pk])
                    if vidx is not None:
                        copy("act", v_all[0:64, s * 512 + 256 * vidx:s * 512 + 256 * (vidx + 1)], pr[0:64, 0:256], [pk], ["vnew"])
            else:
                for b in range(N // 128):
                    pr, pk = galloc()
                    mm_group(pr[:, 0:256], [(nT[:, kc, b * 128:(b + 1) * 128], wv[:, kc, 0:256]) for kc in range(8)], [wk] + NTK, [pk], name="tokp")
                    copy("dve", kvtok[:, b, col0:col0 + 256], pr[:, 0:256], [pk], ["kvtok", pk])
                    if vidx is not None:
                        blk = t0 // 128 + b
                        copy("act", v3[:, blk, 256 * vidx:256 * (vidx + 1)], pr[:, 0:256], [pk], [("v", blk)])

        xload(0, False)
        for ti in range(NT):
            run_tile(ti, False)
        out_dmas.append(dma("pool", stp[:, :, :], carry[:], ["s:carry"] + [("s:carry", 0, j) for j in range(16)] + [("s:carry", 1, j) for j in range(16)], []))
        if with_sample:
            run_tile(NT, True)
            out_dmas.append(dma("pool", sts[:, :, :, :], carry_s[:], ["s:carry_s"] + [("s:carry", si, j) for si in range(2) for j in range(16)], []))


    except _Stop:
        pass

    S.barrier_wait("sp", out_dmas, name="final")

    sems = {e: es.enter_context(nc.semaphore("sem_" + e)) for e in ENGINES}
    dma_sems = {}
    for e in ENGINES:
        for i in range(min(NDMA_RING[e], S.dma_count[e])):
            dma_sems[(e, i)] = es.enter_context(nc.semaphore(f"dsem_{e}_{i}"))
    block = es.enter_context(nc.Block())
    S.assign()

    @block.tensor
    def _(e):
        S.emit_engine("pe", e, sems, dma_sems)

    @block.scalar
    def _(e):
        S.emit_engine("act", e, sems, dma_sems)

    @block.vector
    def _(e):
        S.emit_engine("dve", e, sems, dma_sems)

    @block.gpsimd
    def _(e):
        S.emit_engine("pool", e, sems, dma_sems)

    @block.sync
    def _(e):
        S.emit_engine("sp", e, sems, dma_sems)

    es.close()
    return nc, S


def n_layout(a):
    return np.ascontiguousarray(a.reshape(16, 128).T)


def prep_shared(inp):
    f = np.float32
    sh = {}
    sh["w_in"] = np.ascontiguousarray(inp["w_in"][0], dtype=f)
    sh["w_glu"] = np.ascontiguousarray(inp["w_glu"][0], dtype=f)
    sh["w_ba"] = np.ascontiguousarray(inp["w_branch_attn"][0], dtype=f)
    sh["w_bs"] = np.ascontiguousarray(inp["w_branch_ssm"][0], dtype=f)
    sh["w_out"] = np.ascontiguousarray(inp["w_out"][0], dtype=f)
    sh["w_fg"] = np.ascontiguousarray(inp["w_ffn_gate"][0], dtype=f)
    sh["w_fu"] = np.ascontiguousarray(inp["w_ffn_up"][0], dtype=f)
    sh["w_fd"] = np.ascontiguousarray(inp["w_ffn_down"][0], dtype=f)
    sh["w_pg"] = np.ascontiguousarray(inp["w_ple_gate"][0], dtype=f)
    sh["w_pp"] = np.ascontiguousarray(inp["w_ple_proj"][0], dtype=f)
    gl = [inp[k][0] for k in ("norm_mix_pre", "norm_mix_post", "norm_ffn_pre", "norm_ffn_post", "norm_ple_pre", "norm_ple_post")]
    sh["gains"] = np.ascontiguousarray(np.concatenate([np.asarray(g, f).reshape(8, 128).T for g in gl], axis=1))
    a_re = np.asarray(inp["ssm_a_re"][0], f)
    a_im = np.asarray(inp["ssm_a_im"][0], f)
    ldt = np.repeat(np.asarray(inp["ssm_log_dt"][0], f)[:, None], 64, axis=1)
    sh["ssm_n"] = np.ascontiguousarray(np.stack([n_layout(a_re), n_layout(a_im), n_layout(ldt)], axis=1))
    bpad = np.zeros((128, 2, 4, 512), f)
    for comp, key in enumerate(("ssm_b_re", "ssm_b_im")):
        b = np.asarray(inp[key][0], f)
        for g in range(32):
            m, gl_ = g // 8, g % 8
            bpad[gl_ * 16:(gl_ + 1) * 16, comp, m, gl_ * 64:(gl_ + 1) * 64] = b[g].T
    sh["bpad"] = bpad
    cpad = np.zeros((128, 2, 16, 128), f)
    for comp, key in enumerate(("ssm_c_re", "ssm_c_im")):
        cm = np.asarray(inp[key][0], f)
        for g in range(32):
            j, g2, gl_ = g // 2, g % 2, g % 8
            cpad[g2 * 64:(g2 + 1) * 64, comp, j, gl_ * 16:(gl_ + 1) * 16] = cm[g].T
    sh["cpad"] = cpad
    sh["d_n"] = np.ascontiguousarray(np.asarray(inp["ssm_d"][0], f).reshape(4, 128).T)
    cst = np.zeros((128, 4, 128), f)
    cst[:, 0, :] = np.eye(128, dtype=f)
    jj, ss = np.meshgrid(np.arange(128), np.arange(128), indexing="ij")
    cst[:, 1, :] = np.where(jj >= ss, -1.0, 0.0)
    cst[:, 2, :] = -1.0
    cst[:, 3, :] = np.where(jj < ss, 1.0, 0.0)
    sh["cst"] = cst
    return sh


def prep_core(inp, c):
    f = np.float32
    m = {}
    m["xp"] = np.ascontiguousarray(inp["x_prompt"][c], dtype=f)
    m["xs"] = np.ascontiguousarray(np.asarray(inp["x_sample"][2 * c:2 * c + 2], f).reshape(128, D))
    m["pp"] = np.ascontiguousarray(inp["p_prompt"][0, c], dtype=f)
    m["psm"] = np.ascontiguousarray(np.asarray(inp["p_sample"][0, 2 * c:2 * c + 2], f).reshape(128, PLE))
    m["ck"] = np.ascontiguousarray(np.asarray(inp["cache_k"][0, 2 * c:2 * c + 2], f).reshape(2, PAST, 512))
    m["cv"] = np.ascontiguousarray(np.asarray(inp["cache_v"][0, 2 * c:2 * c + 2], f).reshape(2, PAST, 512))
    s0 = np.zeros((128, 2, 2, 16), f)
    for s in range(2):
        s0[:, s, 0, :] = n_layout(np.asarray(inp["state_ssm_re"][0, 2 * c + s], f))
        s0[:, s, 1, :] = n_layout(np.asarray(inp["state_ssm_im"][0, 2 * c + s], f))
    m["s0"] = s0
    return m


def from_n_layout(a):
    return np.ascontiguousarray(a.T).reshape(32, 64)


_CACHE = {}


def kernel(**inputs):
    inp = {k: np.asarray(v) for k, v in inputs.items()}
    if "nc" not in _CACHE:
        _CACHE["nc"] = build_program()[0]
    nc = _CACHE["nc"]
    sh = prep_shared(inp)
    in_maps = []
    for c in range(NCORES):
        m = dict(sh)
        m.update(prep_core(inp, c))
        in_maps.append(m)
    res = run_bass_kernel_spmd(nc, in_maps, core_ids=list(range(NCORES)))
    R_ = res.results
    f = np.float32
    y_prompt = np.stack([R_[c]["yp"] for c in range(NCORES)]).astype(f)
    y_sample = np.concatenate([R_[c]["ys"].reshape(2, DSEQ, D) for c in range(NCORES)]).astype(f)
    k_prompt = np.stack([R_[c]["kp"].reshape(SEQ, NH, HD) for c in range(NCORES)])[None].astype(f)
    v_prompt = np.stack([R_[c]["vp"].reshape(SEQ, NH, HD) for c in range(NCORES)])[None].astype(f)
    k_sample = np.concatenate([R_[c]["ks"].reshape(2, DSEQ, NH, HD) for c in range(NCORES)])[None].astype(f)
    v_sample = np.concatenate([R_[c]["vs"].reshape(2, DSEQ, NH, HD) for c in range(NCORES)])[None].astype(f)
    srp = np.stack([from_n_layout(R_[c]["stp"][:, :, 0]) for c in range(NCORES)])[None].astype(f)
    sip = np.stack([from_n_layout(R_[c]["stp"][:, :, 1]) for c in range(NCORES)])[None].astype(f)
    srs = np.stack([from_n_layout(R_[c]["sts"][:, s, :, 0]) for c in range(NCORES) for s in range(2)])[None].astype(f)
    sis = np.stack([from_n_layout(R_[c]["sts"][:, s, :, 1]) for c in range(NCORES) for s in range(2)])[None].astype(f)
    return (y_prompt, y_sample, k_prompt, v_prompt, srp, sip, k_sample, v_sample, srs, sis)
```

```python
from contextlib import ExitStack
import math
import os
import numpy as np
import concourse.bass as bass
import concourse.mybir as mybir
from concourse.bass_utils import run_bass_kernel_spmd

F32 = mybir.dt.float32
BF16 = mybir.dt.bfloat16
I32 = mybir.dt.int32
AF = mybir.ActivationFunctionType
ALU = mybir.AluOpType

D = 1024
SEQ = 2048
NH = 8
HD = 64
DFF = 2816
PLE = 256
PAST = 4096
DSEQ = 64
TT = 256
NCORES = 8
EPS = 1e-6

ENGINES = ("pe", "act", "dve", "pool", "sp")
SAME_ENG_WINDOW = 10 ** 9
NDMA_RING = {"pe": 1, "act": 8, "dve": 1, "pool": 16, "sp": 12}


class Op:
    __slots__ = ("eng", "fn", "deps", "inc", "count", "is_dma", "sem", "target", "ring_prev", "name", "idx")

    def __init__(self, eng, fn, is_dma, name):
        self.eng = eng
        self.fn = fn
        self.is_dma = is_dma
        self.deps = []
        self.inc = False
        self.count = None
        self.sem = None
        self.target = None
        self.ring_prev = None
        self.name = name
        self.idx = -1


class Sched:
    def __init__(self):
        self.ops = {e: [] for e in ENGINES}
        self.lastw = {}
        self.readers = {}
        self.dma_count = {e: 0 for e in ENGINES}
        self.dma_hist = {e: [] for e in ENGINES}

    def add(self, eng, fn, reads=(), writes=(), dma=False, name=""):
        op = Op(eng, fn, dma, name)
        op.idx = len(self.ops[eng])
        deps = []
        seen = set()

        def push(d, force=False):
            if d is None or id(d) in seen:
                return
            if (not d.is_dma) and d.eng == eng and eng != "pe" and op.idx - d.idx <= SAME_ENG_WINDOW:
                force = True
            if d.is_dma or dma or d.eng != eng or force:
                seen.add(id(d))
                deps.append(d)

        for k in reads:
            k0 = k[0] if isinstance(k, tuple) else k
            w = self.lastw.get(k)
            near = (w is not None and not w.is_dma and w.eng == eng and eng != "pe" and op.idx - w.idx <= SAME_ENG_WINDOW)
            push(w, force=near or (isinstance(k0, str) and k0.startswith("s:")))
        for k in writes:
            push(self.lastw.get(k))
            for r in self.readers.get(k, ()):
                push(r)
        op.deps = deps
        for d in deps:
            if not d.is_dma:
                d.inc = True
        for k in reads:
            self.readers.setdefault(k, []).append(op)
        for k in writes:
            self.lastw[k] = op
            self.readers[k] = []
        if dma:
            i = self.dma_count[eng]
            self.dma_count[eng] = i + 1
            nr = NDMA_RING[eng]
            op.sem = (eng, i % nr)
            op.target = 16 * (i // nr + 1)
            hist = self.dma_hist[eng]
            if i >= nr:
                op.ring_prev = hist[i - nr]
            hist.append(op)
        self.ops[eng].append(op)
        return op

    def pending(self, keys):
        out = []
        for k in keys:
            w = self.lastw.get(k)
            if w is not None:
                out.append(w)
            out.extend(self.readers.get(k, ()))
        return out

    def barrier_wait(self, eng, ops, name="barrier"):
        op = Op(eng, None, False, name)
        op.deps = [d for d in ops if d.is_dma or d.eng != eng]
        for d in op.deps:
            if not d.is_dma:
                d.inc = True
        self.ops[eng].append(op)
        return op

    def assign(self):
        for e in ENGINES:
            c = 0
            for op in self.ops[e]:
                if op.is_dma:
                    continue
                if op.inc:
                    c += 1
                op.count = c

    def emit_engine(self, e, h, sems, dma_sems):
        waited = {f: 0 for f in ENGINES}
        dwaited = {}
        for op in self.ops[e]:
            dl = list(op.deps)
            if op.ring_prev is not None:
                dl.append(op.ring_prev)
            for d in dl:
                if d.is_dma:
                    if dwaited.get(d.sem, 0) >= d.target:
                        continue
                    dwaited[d.sem] = d.target
                    h.wait_ge(dma_sems[d.sem], d.target)
                else:
                    if waited[d.eng] >= d.count:
                        continue
                    waited[d.eng] = d.count
                    h.wait_ge(sems[d.eng], d.count)
            if op.fn is None:
                continue
            ins = op.fn(h)
            if op.is_dma:
                ins.then_inc(dma_sems[op.sem], 16)
            elif op.inc:
                assert ins is not None, op.name
                ins.then_inc(sems[e], 1)


def weight_blocks():
    blks = []

    def add(name, w, kc0, kcn, c0, ncols):
        blks.append(dict(name=name, w=w, kc0=kc0, kcn=kcn, c0=c0, ncols=ncols))

    for nm, base in (("q", 0), ("k", 512), ("v", 1024), ("u", 1536)):
        for i in range(2):
            add(f"{nm}{i}", "w_in", 0, 8, base + 256 * i, 256)
    add("glu", "w_glu", 0, 4, 0, 512)
    for half in range(2):
        p0 = 2 * half
        add(f"ga{p0}", "w_in", 0, 8, 2048 + 256 * p0, 256)
        add(f"gs{p0}", "w_in", 0, 8, 3072 + 256 * p0, 256)
        add(f"wba{half}", "w_ba", 0, 4, 512 * half, 512)
        add(f"wbs{half}", "w_bs", 0, 4, 512 * half, 512)
        add(f"ga{p0 + 1}", "w_in", 0, 8, 2048 + 256 * (p0 + 1), 256)
        add(f"gs{p0 + 1}", "w_in", 0, 8, 3072 + 256 * (p0 + 1), 256)
    for i in range(4):
        add(f"wo{i}", "w_out", 0, 8, 256 * i, 256)
    for i in range(11):
        add(f"fg{i}", "w_fg", 0, 8, 256 * i, 256)
        add(f"fu{i}", "w_fu", 0, 8, 256 * i, 256)
    for cb in range(4):
        for kg, (k0, kn) in enumerate(((0, 8), (8, 8), (16, 6))):
            add(f"fd{cb}_{kg}", "w_fd", k0, kn, 256 * cb, 256)
    for i in range(4):
        add(f"pg{i}", "w_pg", 0, 8, 256 * i, 256)
    add("pp", "w_pp", 0, 2, 0, 1024)
    return blks


WBLKS = weight_blocks()
WIDX = {b["name"]: i for i, b in enumerate(WBLKS)}
NWB = len(WBLKS)
NSLOT = 6
LOOKAHEAD = 2

W_SHAPES = dict(w_in=(1024, 4096), w_glu=(512, 512), w_ba=(512, 1024), w_bs=(512, 1024),
                w_out=(1024, 1024), w_fg=(1024, 2816), w_fu=(1024, 2816), w_fd=(2816, 1024),
                w_pg=(1024, 1024), w_pp=(256, 1024))


class _Stop(Exception):
    pass


def build_program(NT=8, with_sample=True, dbg=(), stop=None):
    nc = bass.Bass("TRN2", target_bir_lowering=False, dynamic_dma_scratch_size=4096)
    S = Sched()
    es = ExitStack()

    def din(name, shape, dt=F32):
        return nc.dram_tensor(name, list(shape), dt, kind="ExternalInput").ap()

    def dout(name, shape, dt=F32):
        return nc.dram_tensor(name, list(shape), dt, kind="ExternalOutput").ap()

    xp = din("xp", [SEQ, D])
    xs = din("xs", [128, D])
    pp_in = din("pp", [SEQ, PLE])
    ps_in = din("psm", [128, PLE])
    ck = din("ck", [2, PAST, 512])
    cv = din("cv", [2, PAST, 512])
    s0 = din("s0", [128, 2, 2, 16])
    wd = {k: din(k, v) for k, v in W_SHAPES.items()}
    gains_in = din("gains", [128, 48])
    ssmn_in = din("ssm_n", [128, 3, 16])
    bpad_in = din("bpad", [128, 2, 4, 512])
    cpad_in = din("cpad", [128, 2, 16, 128])
    dn_in = din("d_n", [128, 4])
    cst_in = din("cst", [128, 4, 128])
    wscr = {k: nc.dram_tensor("wscr_" + k, list(v), BF16, kind="Internal").ap() for k, v in W_SHAPES.items()}

    yp = dout("yp", [SEQ, D])
    ys = dout("ys", [128, D])
    kp = dout("kp", [SEQ, 512])
    vp = dout("vp", [SEQ, 512])
    ks = dout("ks", [128, 512])
    vs = dout("vs", [128, 512])
    stp = dout("stp", [128, 16, 2])
    sts = dout("sts", [128, 2, 16, 2])
    dbg_out = {}
    out_dmas = []

    def sb(name, shape, dt=F32):
        return es.enter_context(nc.sbuf_tensor(name, list(shape), dt))

    cst_f = sb("cst_f", [128, 4, 128])
    ident_f = cst_f[:, 0, :]
    tmask_f = cst_f[:, 3, :]
    ones_f = sb("ones_f", [128, 128])
    cb_b = sb("cb_b", [128, 4, 128], BF16)
    ident_b = cb_b[:, 0, :]
    tri_b = cb_b[:, 1, :]
    nones_b = cb_b[:, 2, :]
    mb_b = cb_b[:, 3, :]
    ones_b = sb("ones_b", [128, 128], BF16)
    zeros_b = sb("zeros_b", [128, 64], BF16)
    gains = sb("gains_sb", [128, 48])
    d_n = sb("d_n_sb", [128, 4])
    ssmn = sb("ssmn_sb", [128, 3, 16])
    sm = sb("sm", [128, 24, 16])
    smi = sb("smi", [128, 16], I32)
    cs = sb("cs", [128, 16, 2, 128])
    rho_b = sb("rho_b", [128, 16, 128])
    bbT = sb("bbT", [128, 4, 2, 512], BF16)
    Cp = sb("Cp", [128, 16, 2, 128], BF16)
    carry = sb("carry", [128, 16, 2])
    carry_s = sb("carry_s", [128, 2, 16, 2])
    s0_sb = sb("s0_sb", [128, 2, 2, 16])
    ctmp = sb("ctmp", [128, 4])
    csn = sb("csn", [128, 2, 16])
    xin = sb("xin", [128, 2, 1024])
    xT = sb("xT", [128, 8, TT])
    fout = sb("fout", [128, 8, TT])
    kvtok = sb("kvtok", [128, 2, 1024])
    nT = sb("nT", [128, 8, TT], BF16)
    mg = sb("mg", [128, 8, TT], BF16)
    qz = sb("qz", [128, 4, 2, TT], BF16)
    kTn = sb("kTn", [128, 4, 128], BF16)
    uT = sb("uT", [128, 4, TT], BF16)
    du = sb("du", [128, 4, TT])
    pst = du[:, 0:2, :]
    o_att = sb("o_att", [128, 4, TT], BF16)
    o_ssm = sb("o_ssm", [128, 4, TT], BF16)
    sig = sb("sig", [128, 3, TT])
    rstd = sb("rstd", [128, TT])
    NR = 25
    R = sb("R", [128, NR, TT])
    kT_all = sb("kT_all", [128, 4 * SEQ], BF16)
    v_all = sb("v_all", [128, 16 * 512], BF16)
    wsl = sb("wsl", [128, NSLOT, 2048], BF16)
    pT = sb("pT", [128, 2, TT], BF16)

    pbank = [es.enter_context(nc.psum_tensor(f"pb{i}", [128, 512], F32)) for i in range(8)]

    kT3 = kT_all[:].rearrange("p (c t) -> p c t", c=4)
    v3 = v_all[:].rearrange("p (b f) -> p b f", f=512)

    def Rf(i, n=TT):
        return R[:, i, 0:n]

    def Rb(i):
        return R[:, i, :].bitcast(BF16)

    def Rf2(i):
        return R[:, i:i + 2, :].rearrange("p a t -> p (a t)")

    gctr = [0]
    reserved = set()

    def galloc():
        while True:
            b = gctr[0] % 8
            gctr[0] += 1
            if b not in reserved:
                return pbank[b][:, 0:256], ("PB", b)

    def dma(q, out, in_, reads, writes, name=""):
        return S.add(q, lambda e: e.dma_start(out=out, in_=in_), reads=reads, writes=writes, dma=True, name=name)

    def mm_group(out_ap, pairs, reads, writes, name="", start=True, stop=True):
        def fn(e):
            ins = None
            n = len(pairs)
            for i, (l, r) in enumerate(pairs):
                ins = e.matmul(out_ap, lhsT=l, rhs=r, start=(start and i == 0), stop=(stop and i == n - 1))
            return ins
        return S.add("pe", fn, reads=reads, writes=writes, name=name)

    def act(out, in_, func, reads, writes, scale=None, bias=None, name=""):
        kw = {}
        if scale is not None:
            kw["scale"] = scale
        if bias is not None:
            kw["bias"] = bias
        return S.add("act", lambda e: e.activation(out=out, in_=in_, func=func, **kw), reads=reads, writes=writes, name=name)

    def tt(eng, out, in0, in1, op, reads, writes, name=""):
        return S.add(eng, lambda e: e.tensor_tensor(out=out, in0=in0, in1=in1, op=op), reads=reads, writes=writes, name=name)

    def ts(eng, out, in0, s1, op0, reads, writes, s2=None, op1=None, name=""):
        if op1 is None:
            return S.add(eng, lambda e: e.tensor_scalar(out=out, in0=in0, scalar1=s1, scalar2=None, op0=op0), reads=reads, writes=writes, name=name)
        return S.add(eng, lambda e: e.tensor_scalar(out=out, in0=in0, scalar1=s1, scalar2=s2, op0=op0, op1=op1), reads=reads, writes=writes, name=name)

    def stt(out, in0, scalar, in1, op0, op1, reads, writes, name=""):
        return S.add("dve", lambda e: e.scalar_tensor_tensor(out=out, in0=in0, scalar=scalar, in1=in1, op0=op0, op1=op1), reads=reads, writes=writes, name=name)

    def copy(eng, out, in_, reads, writes, name=""):
        if eng == "act":
            return act(out, in_, AF.Copy, reads, writes, name=name)
        return S.add(eng, lambda e: e.tensor_copy(out=out, in_=in_), reads=reads, writes=writes, name=name)

    def memset(eng, ap, val, writes):
        return S.add(eng, lambda e: e.memset(ap, val), writes=writes)

    def tap(name, ap, shape, reads):
        if name not in dbg:
            return
        d = dout("dbg_" + name, shape)
        dbg_out[name] = d
        out_dmas.append(dma("pool", d, ap, reads, [], name="dbg_" + name))

    def checkpoint(name):
        if stop == name:
            raise _Stop()

    evt = [0]

    def evac_eng():
        evt[0] += 1
        return "act" if evt[0] % 2 else "dve"

    try:
        cvt_parts = {}
        for wname in ("w_in", "w_glu", "w_ba", "w_bs", "w_out", "w_fg", "w_fu", "w_fd", "w_pg", "w_pp"):
            rows, cols = W_SHAPES[wname]
            parts = []
            c0 = 0
            while c0 < cols:
                cn = min(2048, cols - c0)
                parts.append((c0, cn))
                c0 += cn
            cvt_parts[wname] = parts
            for pi, (c0, cn) in enumerate(parts):
                dma("pool", wscr[wname][:, c0:c0 + cn], wd[wname][:, c0:c0 + cn], [], [("wscr", wname, pi)], name="cvt")

        wstate = dict(next_load=0)

        def wuse(tile_i, name):
            gi = tile_i * NWB + WIDX[name]
            total = ntiles_total * NWB
            while wstate["next_load"] <= min(gi + LOOKAHEAD, total - 1):
                g = wstate["next_load"]
                li = g % NWB
                b = WBLKS[li]
                n = b["kcn"] * b["ncols"]
                src = wscr[b["w"]].rearrange("(kc p) n -> p kc n", p=128)[:, b["kc0"]:b["kc0"] + b["kcn"], b["c0"]:b["c0"] + b["ncols"]]
                rk = [("wscr", b["w"], pi) for pi, (pc0, pcn) in enumerate(cvt_parts[b["w"]]) if pc0 < b["c0"] + b["ncols"] and b["c0"] < pc0 + pcn]
                dma("sp", wsl[:, g % NSLOT, 0:n].rearrange("p (kc n) -> p kc n", kc=b["kcn"]), src, rk, [("wsl", g % NSLOT)], name="wld")
                wstate["next_load"] += 1
            assert gi > wstate["next_load"] - 1 - NSLOT, (name, gi, wstate["next_load"])
            b = WBLKS[gi % NWB]
            n = b["kcn"] * b["ncols"]
            return wsl[:, gi % NSLOT, 0:n].rearrange("p (kc n) -> p kc n", kc=b["kcn"]), ("wsl", gi % NSLOT)

        ntiles_total = NT + (1 if with_sample else 0)

        dma("sp", cst_f[:], cst_in[:, :, :], [], ["cst"])
        dma("sp", gains[:], gains_in[:, :], [], ["gains"])
        dma("sp", d_n[:], dn_in[:, :], [], ["d_n"])
        dma("sp", ssmn[:], ssmn_in[:, :, :], [], ["s:ssmn"])
        dma("sp", s0_sb[:], s0[:, :, :, :], [], ["s:s0"])
        copy("dve", cb_b[:, 0:3, :], cst_f[:, 0:3, :], ["cst"], ["cstb"])
        ts("dve", mb_b, cst_f[:, 1, :], 30000.0, ALU.mult, ["cst"], ["cstb"])
        memset("dve", qz[:].rearrange("p c h t -> p (c h t)"), 0.0, ["qz"])
        memset("dve", ones_f[:], 1.0, ["ones_f"])
        memset("dve", ones_b[:], 1.0, ["ones_b"])
        memset("dve", zeros_b[:], 0.0, ["zeros_b"])
        memset("dve", carry[:], 0.0, ["s:carry"])

        checkpoint("s0")
        cre_t = fout[:].rearrange("p a t -> p (a t)")
        cim_t = xT[:].rearrange("p a t -> p (a t)")
        FO = [("fout", c) for c in range(8)]
        XT = [("xT", c) for c in range(8)]
        dma("sp", cre_t.rearrange("p (j f) -> p j f", j=16), cpad_in[:, 0, :, :], [], FO)
        dma("sp", cim_t.rearrange("p (j f) -> p j f", j=16), cpad_in[:, 1, :, :], [], XT)
        copy("dve", Cp[:, :, 0, :], cre_t.rearrange("p (j f) -> p j f", j=16), FO, ["Cp"])
        ts("dve", Cp[:, :, 1, :], cim_t.rearrange("p (j f) -> p j f", j=16), -1.0, ALU.mult, XT, ["Cp"])

        checkpoint("s1")
        A_RE, A_IM, LDT = ssmn[:, 0, :], ssmn[:, 1, :], ssmn[:, 2, :]
        names = ["dt", "ar", "ai", "rho", "y", "kf", "fr", "s1", "s2", "c1", "sin", "cos", "lbr", "lbi", "nr",
                 "den", "rden", "fre", "fim", "t0", "t1", "t2", "t3", "t4"]
        V = {n: sm[:, i, :] for i, n in enumerate(names)}

        def K(n):
            return "s:" + n

        act(V["dt"], LDT, AF.Exp, ["s:ssmn"], [K("dt")])
        tt("dve", V["ar"], A_RE, V["dt"], ALU.mult, ["s:ssmn", K("dt")], [K("ar")])
        tt("dve", V["ai"], A_IM, V["dt"], ALU.mult, ["s:ssmn", K("dt")], [K("ai")])
        act(V["rho"], V["ar"], AF.Exp, [K("ar")], [K("rho")])
        ts("dve", V["y"], V["ai"], 1.0 / (2.0 * math.pi), ALU.mult, [K("ai")], [K("y")])
        copy("dve", smi[:], V["y"], [K("y")], [K("ki")])
        copy("dve", V["kf"], smi[:], [K("ki")], [K("kf")])
        tt("dve", V["fr"], V["y"], V["kf"], ALU.subtract, [K("y"), K("kf")], [K("fr")])
        act(V["s1"], V["fr"], AF.Sin, [K("fr")], [K("s1")], scale=math.pi)
        act(V["s2"], V["fr"], AF.Sin, [K("fr")], [K("s2")], scale=math.pi / 2.0)
        tt("dve", V["t0"], V["s2"], V["s2"], ALU.mult, [K("s2")], [K("t0")])
        ts("dve", V["c1"], V["t0"], -2.0, ALU.mult, [K("t0")], [K("c1")], s2=1.0, op1=ALU.add)
        tt("dve", V["t1"], V["s1"], V["c1"], ALU.mult, [K("s1"), K("c1")], [K("t1")])
        ts("dve", V["sin"], V["t1"], 2.0, ALU.mult, [K("t1")], [K("sin")])
        tt("dve", V["t2"], V["s1"], V["s1"], ALU.mult, [K("s1")], [K("t2")])
        ts("dve", V["cos"], V["t2"], -2.0, ALU.mult, [K("t2")], [K("cos")], s2=1.0, op1=ALU.add)
        tt("dve", V["lbr"], V["rho"], V["cos"], ALU.mult, [K("rho"), K("cos")], [K("lbr")])
        tt("dve", V["lbi"], V["rho"], V["sin"], ALU.mult, [K("rho"), K("sin")], [K("lbi")])
        ts("dve", V["nr"], V["lbr"], -1.0, ALU.add, [K("lbr")], [K("nr")])
        tt("dve", V["t3"], A_RE, A_RE, ALU.mult, ["s:ssmn"], [K("t3")])
        tt("dve", V["t4"], A_IM, A_IM, ALU.mult, ["s:ssmn"], [K("t4")])
        tt("dve", V["den"], V["t3"], V["t4"], ALU.add, [K("t3"), K("t4")], [K("den")])
        S.add("dve", lambda e: e.reciprocal(out=V["rden"], in_=V["den"]), reads=[K("den")], writes=[K("rden")])
        tt("dve", V["t0"], V["nr"], A_RE, ALU.mult, [K("nr"), "s:ssmn"], [K("t0")])
        tt("dve", V["t1"], V["lbi"], A_IM, ALU.mult, [K("lbi"), "s:ssmn"], [K("t1")])
        tt("dve", V["t2"], V["t0"], V["t1"], ALU.add, [K("t0"), K("t1")], [K("t2")])
        tt("dve", V["fre"], V["t2"], V["rden"], ALU.mult, [K("t2"), K("rden")], [K("fre")])
        tt("dve", V["t3"], V["lbi"], A_RE, ALU.mult, [K("lbi"), "s:ssmn"], [K("t3")])
        tt("dve", V["t4"], V["nr"], A_IM, ALU.mult, [K("nr"), "s:ssmn"], [K("t4")])
        tt("dve", V["t0"], V["t3"], V["t4"], ALU.subtract, [K("t3"), K("t4")], [K("t0")])
        tt("dve", V["fim"], V["t0"], V["rden"], ALU.mult, [K("t0"), K("rden")], [K("fim")])

        checkpoint("s2")
        RK_ALL = [("R", i) for i in range(NR)] + FO + XT
        tabT = R[:, 0:16, :].rearrange("p a t -> p (a t)").rearrange("p (c t j) -> p c t j", c=2, t=128)
        tA = fout[:].rearrange("p a t -> p (a t)")[:, 0:1024].rearrange("p (t j) -> p t j", j=16)
        tB = xT[:].rearrange("p a t -> p (a t)")[:, 0:1024].rearrange("p (t j) -> p t j", j=16)
        S.add("dve", lambda e: e.memset(R[:, 16, :], 0.0), reads=[], writes=RK_ALL)
        copy("dve", tabT[:, 0, 0, :], V["cos"], [K("cos"), ("R", 16)], ["s:tab"])
        copy("dve", tabT[:, 1, 0, :], V["sin"], [K("sin")], ["s:tab"])
        k = 1
        while k < 128:
            cr = tabT[:, 0, k - 1:k, :].to_broadcast([128, k, 16])
            ci = tabT[:, 1, k - 1:k, :].to_broadcast([128, k, 16])
            are = tabT[:, 0, 0:k, :]
            aim = tabT[:, 1, 0:k, :]
            tt("dve", tA[:, 0:k, :], are, cr, ALU.mult, ["s:tab"], ["s:tA"])
            tt("dve", tB[:, 0:k, :], aim, ci, ALU.mult, ["s:tab"], ["s:tB"])
            tt("dve", tabT[:, 0, k:2 * k, :], tA[:, 0:k, :], tB[:, 0:k, :], ALU.subtract, ["s:tA", "s:tB"], ["s:tab2"])
            tt("dve", tA[:, 0:k, :], are, ci, ALU.mult, ["s:tab", "s:tab2"], ["s:tA"])
            tt("dve", tB[:, 0:k, :], aim, cr, ALU.mult, ["s:tab", "s:tab2"], ["s:tB"])
            tt("dve", tabT[:, 1, k:2 * k, :], tA[:, 0:k, :], tB[:, 0:k, :], ALU.add, ["s:tA", "s:tB"], ["s:tab"])
            k *= 2
        for j in range(16):
            for comp in range(2):
                copy("dve" if comp == 0 else "act", cs[:, j, comp, :], tabT[:, comp, :, j], ["s:tab", "s:tab2"], ["s:cs"])
            ts("dve", rho_b[:, j, :], ones_f[:, 0:128], V["rho"][:, j:j + 1], ALU.mult, ["ones_f", K("rho")], ["rho_b"])
        ts("dve", csn[:, 0, :], cs[:, :, 1, 63], -1.0, ALU.mult, ["s:cs"], ["s:csn"])
        ts("dve", csn[:, 1, :], cs[:, :, 1, 127], -1.0, ALU.mult, ["s:cs"], ["s:csn"])
        S.add("dve", lambda e: e.memset(R[:, 16, :], 0.0), reads=["s:cs"], writes=RK_ALL)

        checkpoint("s3")
        Fre = R[:, 0:8, :].rearrange("p a t -> p (a t)")
        Fim = R[:, 8:16, :].rearrange("p a t -> p (a t)")
        diag = sig[:, 0:2, :].rearrange("p a t -> p (a t)")
        for comp, (fv, Fdst) in enumerate(((V["fre"], Fre), (V["fim"], Fim))):
            for q4 in range(4):
                bank = pbank[q4 % 4]
                for jj in range(4):
                    j = q4 * 4 + jj
                    dsl = diag[:, (jj % 4) * 128:(jj % 4 + 1) * 128]
                    ts("dve", dsl, ident_f, fv[:, j:j + 1], ALU.mult, ["cst", K("fre"), K("fim")], [("sig", jj // 2)])
                    mm_group(bank[:, jj * 128:(jj + 1) * 128], [(ones_f[:], dsl)], [("sig", jj // 2), "ones_f"], [("PB", q4 % 4)])
                copy("dve", Fdst[:, q4 * 512:(q4 + 1) * 512], bank[:, :], [("PB", q4 % 4)], [("R", 8 * comp + 2 * q4), ("R", 8 * comp + 2 * q4 + 1)])
        checkpoint("s4")
        Bre = xin[:].rearrange("p a f -> p (a f)").rearrange("p (m n) -> p m n", m=4)
        Bim = kvtok[:].rearrange("p a f -> p (a f)").rearrange("p (m n) -> p m n", m=4)
        dma("sp", Bre, bpad_in[:, 0, :, :], [], ["xin"])
        dma("sp", Bim, bpad_in[:, 1, :, :], [], ["kvtok"])
        Fre4 = Fre.rearrange("p (m n) -> p m n", m=4)
        Fim4 = Fim.rearrange("p (m n) -> p m n", m=4)
        t1v = fout[:].rearrange("p a t -> p (a t)").rearrange("p (m n) -> p m n", m=4)
        t2v = xT[:].rearrange("p a t -> p (a t)").rearrange("p (m n) -> p m n", m=4)
        RK8a = [("R", i) for i in range(8)]
        RK8b = [("R", i) for i in range(8, 16)]
        tt("dve", t1v, Fre4, Bre, ALU.mult, RK8a + ["xin", "Cp"], FO)
        tt("dve", t2v, Fim4, Bim, ALU.mult, RK8b + ["kvtok", "Cp"], XT)
        tt("dve", bbT[:, :, 0, :], t1v, t2v, ALU.subtract, FO + XT, ["bbT"])
        tt("dve", t1v, Fre4, Bim, ALU.mult, RK8a + ["kvtok"], FO)
        tt("dve", t2v, Fim4, Bre, ALU.mult, RK8b + ["xin"], XT)
        tt("dve", bbT[:, :, 1, :], t1v, t2v, ALU.add, FO + XT, ["bbT"])
        tap("bbT", bbT[:].rearrange("p a b c -> p (a b c)"), [128, 4096], ["bbT"]) if False else None
        checkpoint("s5")
        copy("dve", carry_s[:, :, :, 0], s0_sb[:, :, 0, :], ["s:s0"], ["s:carry_s"])
        copy("dve", carry_s[:, :, :, 1], s0_sb[:, :, 1, :], ["s:s0"], ["s:carry_s"])

        def rmsnorm(N, src3, src_keys, gidx, mode, tagq):
            MGK = [("mg", c) for c in range(8)]
            act(mg[:, :, 0:N], src3, AF.Square, list(src_keys), MGK)
            pr, pk = galloc()
            mm_group(pr[:, 0:N], [(ones_b[:], mg[:, c, 0:N]) for c in range(8)], MGK + ["ones_b"], [pk])
            act(sig[:, 2, 0:N], pr[:, 0:N], AF.Ln, [pk], [("sig", 2)], scale=1.0 / D, bias=EPS)
            act(rstd[:, 0:N], sig[:, 2, 0:N], AF.Exp, [("sig", 2)], ["rstd"], scale=-0.5)
            if mode == "bf":
                for c in range(8):
                    g = gains[:, gidx * 8 + c:gidx * 8 + c + 1]
                    stt(nT[:, c, 0:N], src3[:, c, :], g, rstd[:, 0:N], ALU.mult, ALU.mult, [src_keys[c], "rstd", "gains"], [("nT", c)])
            else:
                tt("dve", src3, src3, rstd[:, 0:N].unsqueeze(1).to_broadcast([128, 8, N]), ALU.mult, list(src_keys) + ["rstd"], list(src_keys))
                for c in range(8):
                    g = gains[:, gidx * 8 + c:gidx * 8 + c + 1]
                    stt(xT[:, c, 0:N], src3[:, c, :], g, xT[:, c, 0:N], ALU.mult, ALU.add, [src_keys[c], ("xT", c), "gains"], [("xT", c)])

        NTK = [("nT", c) for c in range(8)]

        def fm_proj(ti, N, wname, rhs_buf, rhs_keys, nkc, ncol_chunks, consume):
            wv, wk = wuse(ti, wname)
            for jj in range(ncol_chunks):
                pr, pk = galloc()
                mm_group(pr[:, 0:N], [(wv[:, kc, jj * 128:(jj + 1) * 128], rhs_buf[:, kc, 0:N]) for kc in range(nkc)],
                         [wk] + list(rhs_keys), [pk], name=wname)
                consume(jj, pr[:, 0:N], pk)

        def ssm_gen(ti, N, segs, carry_of, ybank=4, carry_on_act=False):
            T = segs[0][1]
            nseg = len(segs)
            Ybank = pbank[ybank]
            YK = ("PB", ybank)

            def v3d(ap):
                return ap.rearrange("p (s t) -> p s t", t=T)

            for jp in range(8):
                ctxs = []
                for u_ in range(2):
                    j = 2 * jp + u_
                    m, jj = j // 4, j % 4
                    c = dict(j=j, m=m, jj=jj,
                             cosT=cs[:, j, 0, 0:T].unsqueeze(1).to_broadcast([128, nseg, T]),
                             sinT=cs[:, j, 1, 0:T].unsqueeze(1).to_broadcast([128, nseg, T]))
                    if u_ == 0:
                        c.update(xre=Rf(4, N), xim=Rf(5, N), kx0=("R", 4), kx1=("R", 5), qre=Rf(6, N), qim=Rf(7, N), kq0=("R", 6), kq1=("R", 7),
                                 ct=ctmp[:, 0:2], kct="s:ctA", sl=8)
                    else:
                        c.update(xre=sig[:, 2, 0:N], xim=rstd[:, 0:N], kx0=("sig", 2), kx1="rstd", qre=Rf(23, N), qim=Rf(24, N),
                                 kq0=("R", 23), kq1=("R", 24), ct=ctmp[:, 2:4], kct="s:ctB", sl=9)
                    ctxs.append(c)

                def stage_x(c):
                    m, jj = c["m"], c["jj"]
                    pa, pka = galloc()
                    pb, pkb = galloc()
                    mm_group(pa[:, 0:N], [(bbT[:, m, 0, jj * 128:(jj + 1) * 128], uT[:, m, 0:N])], ["bbT", ("uT", m)], [pka], name="bu_re")
                    mm_group(pb[:, 0:N], [(bbT[:, m, 1, jj * 128:(jj + 1) * 128], uT[:, m, 0:N])], ["bbT", ("uT", m)], [pkb], name="bu_im")
                    c.update(pka=pka, pkb=pkb, bre=v3d(pa[:, 0:N]), bim=v3d(pb[:, 0:N]))
                    tt("dve", v3d(Rf(0, N)), c["bre"], c["cosT"], ALU.mult, [c["pka"], "s:cs"], [("R", 0)])
                    tt("dve", v3d(Rf(2, N)), c["bim"], c["cosT"], ALU.mult, [c["pkb"], "s:cs"], [("R", 2)])
                    tt("dve", v3d(Rf(1, N)), c["bim"], c["sinT"], ALU.mult, [c["pkb"], "s:cs"], [("R", 1)])
                    tt("dve", v3d(Rf(3, N)), c["bre"], c["sinT"], ALU.mult, [c["pka"], "s:cs"], [("R", 3)])
                    tt("dve", c["xre"], Rf(0, N), Rf(1, N), ALU.add, [("R", 0), ("R", 1)], [c["kx0"]])
                    tt("dve", c["xim"], Rf(2, N), Rf(3, N), ALU.subtract, [("R", 2), ("R", 3)], [c["kx1"]])

                def stage_scan(c, si):
                    c0, Ts = segs[si]
                    j = c["j"]
                    cre, cim = carry_of(si)(j)
                    ckey = ("s:carry", si, j)
                    S.add("dve", lambda e, c0=c0, Ts=Ts, cre=cre, j=j, qre=c["qre"], xre=c["xre"]: e.tensor_tensor_scan(
                        out=qre[:, c0:c0 + Ts], data0=rho_b[:, j, 0:Ts], data1=xre[:, c0:c0 + Ts], initial=cre,
                        op0=ALU.mult, op1=ALU.add), reads=[c["kx0"], "rho_b", "s:carry", "s:carry_s", ckey], writes=[c["kq0"]])
                    S.add("dve", lambda e, c0=c0, Ts=Ts, cim=cim, j=j, qim=c["qim"], xim=c["xim"]: e.tensor_tensor_scan(
                        out=qim[:, c0:c0 + Ts], data0=rho_b[:, j, 0:Ts], data1=xim[:, c0:c0 + Ts], initial=cim,
                        op0=ALU.mult, op1=ALU.add), reads=[c["kx1"], "rho_b", "s:carry", "s:carry_s", ckey], writes=[c["kq1"]])

                def stage_carry(c, si):
                    c0, Ts = segs[si]
                    j = c["j"]
                    cre, cim = carry_of(si)(j)
                    ckey = ("s:carry", si, j)
                    cT = cs[:, j, 0, Ts - 1:Ts]
                    sT = cs[:, j, 1, Ts - 1:Ts]
                    ql_re = c["qre"][:, c0 + Ts - 1:c0 + Ts]
                    ql_im = c["qim"][:, c0 + Ts - 1:c0 + Ts]
                    ct = c["ct"]
                    if carry_on_act:
                        nsT = csn[:, 0 if Ts == 64 else 1, j:j + 1]
                        act(ct[:, 0:1], ql_im, AF.Copy, [c["kq0"], c["kq1"], "s:csn"], [c["kct"] + "0"], scale=nsT)
                        act(ct[:, 1:2], ql_re, AF.Copy, [c["kq0"], c["kq1"], "s:cs"], [c["kct"] + "1"], scale=sT)
                        act(cre, ql_re, AF.Identity, [c["kq0"], c["kq1"], c["kct"] + "0", "s:cs"], [ckey], scale=cT, bias=ct[:, 0:1])
                        act(cim, ql_im, AF.Identity, [c["kq0"], c["kq1"], c["kct"] + "1", "s:cs"], [ckey], scale=cT, bias=ct[:, 1:2])
                    else:
                        ts("dve", ct[:, 0:1], ql_im, sT, ALU.mult, [c["kq0"], c["kq1"], "s:cs"], [c["kct"] + "0"])
                        ts("dve", ct[:, 1:2], ql_re, sT, ALU.mult, [c["kq0"], c["kq1"], "s:cs"], [c["kct"] + "1"])
                        stt(cre, ql_re, cT, ct[:, 0:1], ALU.mult, ALU.subtract, [c["kq0"], c["kq1"], c["kct"] + "0", "s:cs"], [ckey])
                        stt(cim, ql_im, cT, ct[:, 1:2], ALU.mult, ALU.add, [c["kq0"], c["kq1"], c["kct"] + "1", "s:cs"], [ckey])

                def stage_pool(c):
                    j, m, jj, sl = c["j"], c["m"], c["jj"], c["sl"]
                    sbf = Rb(sl)[:, 0:2 * N].rearrange("p (a t) -> p a t", a=2)
                    qre, qim = c["qre"], c["qim"]
                    tt("pool", v3d(Rf(21, N)), v3d(qre), c["cosT"], ALU.mult, [c["kq0"], "s:cs"], [("R", 21)])
                    tt("pool", v3d(Rf(22, N)), v3d(qim), c["sinT"], ALU.mult, [c["kq1"], "s:cs"], [("R", 22)])
                    tt("pool", sbf[:, 0, :], Rf(21, N), Rf(22, N), ALU.subtract, [("R", 21), ("R", 22)], [("R", sl)])
                    tt("pool", v3d(Rf(21, N)), v3d(qim), c["cosT"], ALU.mult, [c["kq1"], "s:cs"], [("R", 21)])
                    tt("pool", v3d(Rf(22, N)), v3d(qre), c["sinT"], ALU.mult, [c["kq0"], "s:cs"], [("R", 22)])
                    tt("pool", sbf[:, 1, :], Rf(21, N), Rf(22, N), ALU.add, [("R", 21), ("R", 22)], [("R", sl)])
                    mm_group(Ybank[:, 0:N], [(Cp[:, j, 0, :], sbf[:, 0, :]), (Cp[:, j, 1, :], sbf[:, 1, :])], [("R", sl), "Cp"], [YK],
                             start=(jj == 0), stop=(jj == 3), name="Cproj")

                A, B = ctxs
                stage_x(A); yield
                stage_x(B); yield
                for si in range(nseg):
                    stage_scan(A, si); yield
                    stage_scan(B, si); yield
                    stage_carry(A, si); yield
                    stage_carry(B, si); yield
                stage_pool(A); yield
                stage_pool(B); yield
                if B["jj"] == 3:
                    m = B["m"]
                    y = Rf(10, N)
                    tt("dve", y, Ybank[:, 0:N], du[:, m, 0:N], ALU.add, [YK, ("du", m)], [("R", 10)])
                    tt("dve", Rf(11, N), y, y, ALU.mult, [("R", 10)], [("R", 11)])
                    yield
                    ts("dve", Rf(11, N), Rf(11, N), 0.044715, ALU.mult, [("R", 11)], [("R", 11)], s2=1.0, op1=ALU.add)
                    yield
                    tt("dve", Rf(11, N), Rf(11, N), y, ALU.mult, [("R", 11), ("R", 10)], [("R", 11)])
                    act(sig[:, 0, 0:N], Rf(11, N), AF.Sigmoid, [("R", 11)], [("sig", 0)], scale=1.5957691216057308)
                    yield
                    tt("dve", Rf(12 + m, N), y, sig[:, 0, 0:N], ALU.mult, [("R", 10), ("sig", 0)], [("R", 12 + m)])
                    copy("pool", mg[:, m, 0:N], Rf(12 + m, N), [("R", 12 + m)], [("mg", m)])
                    yield

        def ssm_glu(ti, N):
            def cons(jo, pr, pk):
                act(sig[:, jo % 2, 0:N], pr, AF.Sigmoid, [pk], [("sig", jo % 2)])
                tt("dve", o_ssm[:, jo, 0:N], Rf(12 + jo, N), sig[:, jo % 2, 0:N], ALU.mult, [("R", 12 + jo), ("sig", jo % 2)], [("o_ssm", jo)])
            fm_proj(ti, N, "glu", mg, [("mg", c) for c in range(4)], 4, 4, cons)

        def attn_prompt(ti, N, pump):
            t0 = ti * TT
            kmax = (t0 + N) // 128 - 1
            kbs = list(range(kmax, -1, -1))
            nkb = len(kbs)
            nones_f = cst_f[:, 2, :]
            for h in range(NH):
                c, half, hp = h // 2, h % 2, 64 * (h % 2)
                Ob = pbank[5 + (h % 2)]
                Okey = ("PB", 5 + (h % 2))
                Oap = Ob[hp:hp + 64, 0:N]
                mm_group(Oap, [(zeros_b[:, 0:64], qz[:, 0, 0, 0:N])], ["zeros_b", ("qz", 0)], [Okey], start=True, stop=False, name="Ozero")
                memset("pool", Rf(19, N), 0.0, [("R", 19)])

                def c0_of(kb):
                    return max(0, kb * 128 - t0)

                def lp_of(par):
                    return Rb(18)[:, par * 256:(par + 1) * 256]

                def w_of(par):
                    return Rb(20)[:, par * 256:(par + 1) * 256]

                def zmm(e, out, kb, c0, first, last, c=c, half=half):
                    diag = kb * 128 >= t0
                    ins = e.matmul(out[:, c0:N], lhsT=kT3[:, c, kb * 128:(kb + 1) * 128], rhs=qz[:, c, half, c0:N],
                                   start=first, stop=(last and not diag))
                    if diag:
                        ins = e.matmul(out[:, c0:c0 + 128], lhsT=ident_b, rhs=mb_b, start=False, stop=last)
                    return ins

                def stage1a(i):
                    kb = kbs[i]
                    c0 = c0_of(kb)
                    par = i % 2
                    pz, pzk = galloc()
                    S.add("pe", lambda e, pz=pz, kb=kb, c0=c0, zmm=zmm: zmm(e, pz, kb, c0, True, True),
                          reads=[("kT", kb // 2), ("qz", c), "cstb"], writes=[pzk], name="QK")
                    act(Rf(16 + par, N)[:, c0:N], pz[:, c0:N], AF.Exp, [pzk], [("R", 16 + par)])

                def stage1b(i):
                    kb = kbs[i]
                    c0 = c0_of(kb)
                    par = i % 2
                    act(lp_of(par)[:, c0:N], Rf(16 + par, N)[:, c0:N], AF.Ln, [("R", 16 + par)], [("R", 18, par)], bias=1.0)

                def stage2pre(i):
                    kb = kbs[i]
                    c0 = c0_of(kb)
                    psr, psk = galloc()
                    psr_of[i] = (psr, psk)
                    reserved.add(psk[1])

                    def sf1(e, i=i, kb=kb, c0=c0, psr=psr, zmm=zmm):
                        ins = zmm(e, psr, kb, c0, True, False)
                        if i > 0:
                            ins = e.matmul(psr[:, c0:N], lhsT=nones_f, rhs=Rf(19, N)[:, c0:N], start=False, stop=False)
                        return ins
                    S.add("pe", sf1, reads=["cstb", "cst", ("kT", kb // 2), ("qz", c)] + ([("R", 19)] if i > 0 else []),
                          writes=[psk], name="S1")

                def stage2(i):
                    kb = kbs[i]
                    c0 = c0_of(kb)
                    par = i % 2
                    lp = lp_of(par)
                    psr, psk = psr_of.pop(i)
                    S.add("pe", lambda e, c0=c0, lp=lp, psr=psr: e.matmul(psr[:, c0:N], lhsT=tri_b, rhs=lp[:, c0:N], start=False, stop=True),
                          reads=[("R", 18, par), "cstb"], writes=[psk], name="S2")
                    act(w_of(par)[:, c0:N], psr[:, c0:N], AF.Exp, [psk], [("R", 20, par)])
                    reserved.discard(psk[1])
                    if i + 1 < nkb:
                        tt("pool", Rf(19, N)[:, c0:N], Rf(19, N)[:, c0:N], lp[:, c0:N], ALU.add, [("R", 19), ("R", 18, par)], [("R", 19)])

                def stage3(i):
                    kb = kbs[i]
                    c0 = c0_of(kb)
                    par = i % 2
                    w = w_of(par)
                    mm_group(Ob[hp:hp + 64, c0:N], [(v3[:, kb, h * 64:(h + 1) * 64], w[:, c0:N])], [("v", kb), ("R", 20, par)], [Okey],
                             start=False, stop=(i == nkb - 1), name="WV")

                psr_of = {}
                for step in range(nkb + 2):
                    if step < nkb:
                        stage1a(step)
                    if 0 <= step - 1 < nkb:
                        stage2(step - 1)
                    if step < nkb:
                        stage1b(step)
                        stage2pre(step)
                    if 0 <= step - 2 < nkb:
                        stage3(step - 2)
                    pump()
                copy("act", o_att[hp:hp + 64, c, 0:N], Oap, [Okey], [("o_att", c)])

        def attn_sample(ti, pump):
            old = [("kT", i) for i in range(8)] + [("v", i) for i in range(16)]
            pend = S.pending(old)
            for e in ("pe", "act", "dve", "pool"):
                S.barrier_wait(e, pend, name="kvfence")
            NST = 3
            kblk = [kT_all[:, i * 512:(i + 1) * 512] for i in range(NST)]
            kTb = [kT_all[:, 2048 + i * 512:2048 + (i + 1) * 512].rearrange("p (c t) -> p c t", c=4) for i in range(2)]
            NSTV = 5
            vblk = [v_all[:, 2048 + i * 512:2048 + (i + 1) * 512] for i in range(NSTV)]
            vnew = v_all[:, 0:1024].rearrange("p (s f) -> p s f", s=2)
            Oacc = v_all[:, 6656:7168].bitcast(F32)
            reserved.update((4, 5, 6, 7))
            UK = [("qz", c) for c in range(4)]
            for s in range(2):
                e32 = [kT_all[:, 3072:4096].bitcast(F32), kT_all[:, 4096:5120].bitcast(F32)]
                exs = [kT_all[:, 5120:6144].bitcast(F32), kT_all[:, 6144:7168].bitcast(F32)]
                lpb = [kT_all[:, 7168:7680], kT_all[:, 7680:8192]]
                ls32 = v_all[:, 4608:5632].bitcast(F32)
                wb = [v_all[:, 5632:6144], v_all[:, 6144:6656]]
                ke32 = [[("sa", 0)], [("sa", 1)]]
                kls = [("sa", 6)]
                kex = [[("sa", 2)], [("sa", 3)]]
                memset("pool", ls32, 0.0, kls)
                blocks = ["new"] + list(range(PAST // 128 - 1, -1, -1))
                nb = len(blocks)
                zb = [pbank[4], pbank[5]]
                sbk = [pbank[6], pbank[7]]

                def load(i):
                    b = blocks[i]
                    if b == "new":
                        return
                    sl = i % NST
                    dma("pool", kblk[sl], ck[s, b * 128:(b + 1) * 128, :], [], [("kblk", sl)], name="kld")
                    dma("pool", vblk[i % NSTV], cv[s, b * 128:(b + 1) * 128, :], [], [("vblk", i % NSTV)], name="vld")

                def stage1(i):
                    b = blocks[i]
                    par = i % 2
                    Z = zb[par]
                    zk = ("PB", 4 + par)
                    if b == "new":
                        KP = 64
                        prs = []
                        for c in range(4):
                            prs.append((kTn[:, c, s * 64:(s + 1) * 64], qz[:, c, :, s * 64:(s + 1) * 64]))
                        rd = ["kTn"] + UK
                    else:
                        KP = 128
                        sl = i % NST
                        pr, pk = galloc()
                        prb = pr.bitcast(BF16)
                        def trf(e, sl=sl, prb=prb):
                            ins = None
                            for c in range(4):
                                ins = e.transpose(out=prb[:, c * 128:(c + 1) * 128], in_=kblk[sl][:, c * 128:(c + 1) * 128], identity=ident_b)
                            return ins
                        S.add("pe", trf, reads=[("kblk", sl), "cstb"], writes=[pk], name="ktr")
                        copy("dve", kTb[par][:].rearrange("p c t -> p (c t)"), prb, [pk], [("kTb", par)])
                        prs = []
                        for c in range(4):
                            prs.append((kTb[par][:, c, :], qz[:, c, :, s * 64:(s + 1) * 64]))
                        rd = [("kTb", par)] + UK

                    def zf(e, prs=prs, Z=Z, KP=KP):
                        ins = None
                        for c, (l, r) in enumerate(prs):
                            ins = e.matmul(Z[0:KP, c * 128:(c + 1) * 128].rearrange("p (h t) -> p h t", h=2), lhsT=l, rhs=r, start=True, stop=True)
                        return ins
                    S.add("pe", zf, reads=rd, writes=[zk], name="QKs")
                    act(e32[par][0:KP, :], Z[0:KP, :], AF.Exp, [zk], ke32[par])
                    if b == "new":
                        ev = e32[par][0:64, :].rearrange("p (h t) -> p h t", h=NH)
                        tt("dve", ev, ev, tmask_f[0:64, 0:64].unsqueeze(1).to_broadcast([64, NH, 64]), ALU.mult, ke32[par] + ["cst"], ke32[par])
                    act(lpb[par][0:KP, :], e32[par][0:KP, :], AF.Ln, ke32[par], [("sa", 4 + par)], bias=1.0)

                def stage2(i):
                    b = blocks[i]
                    par = i % 2
                    KP = 64 if b == "new" else 128
                    SP = sbk[par]
                    sk = ("PB", 6 + par)
                    pairs = [(tri_b[0:KP, 0:KP], lpb[par][0:KP, :])]
                    rd = [("sa", 4 + par), "cstb"]
                    if i > 0:
                        pairs.append((cst_f[:, 2, 0:KP], ls32[:, :]))
                        rd += kls + ["cst"]
                    mm_group(SP[0:KP, :], pairs, rd, [sk], name="Ss")
                    if i + 1 < nb:
                        tt("dve", ls32[0:KP, :], ls32[0:KP, :], lpb[par][0:KP, :], ALU.add, kls + [("sa", 4 + par)], kls)
                    act(exs[par][0:KP, :], SP[0:KP, :], AF.Exp, [sk], kex[par])
                    tt("dve", wb[par][0:KP, :], e32[par][0:KP, :], exs[par][0:KP, :], ALU.mult, ke32[par] + kex[par], [("sa", 7 + par)])

                def stage3(i):
                    b = blocks[i]
                    par = i % 2
                    KP = 64 if b == "new" else 128
                    pr, pk = galloc()
                    prs = []
                    for h in range(NH):
                        c, hp = h // 2, 64 * (h % 2)
                        if b == "new":
                            vsrc = vnew[0:64, s, h * 64:(h + 1) * 64]
                        else:
                            vsrc = vblk[i % NSTV][:, h * 64:(h + 1) * 64]
                        prs.append((pr[hp:hp + 64, c * 64:(c + 1) * 64], vsrc, wb[par][0:KP, h * 64:(h + 1) * 64]))

                    def wvf(e, prs=prs):
                        ins = None
                        for (o, l, r) in prs:
                            ins = e.matmul(o, lhsT=l, rhs=r, start=True, stop=True)
                        return ins
                    rd = [("sa", 7 + par)] + (["vnew"] if b == "new" else [("vblk", i % NSTV)])
                    S.add("pe", wvf, reads=rd, writes=[pk], name="WVs")
                    if i == 0:
                        copy("dve", Oacc, pr[:, :], [pk], [("sa", 9)])
                    else:
                        tt("dve", Oacc, pr[:, :], Oacc, ALU.add, [pk, ("sa", 9)], [("sa", 9)])

                load(0); load(1)
                for step in range(nb + 2):
                    checkpoint(f"SE_{s}_{step}")
                    if step + 2 < nb:
                        load(step + 2)
                    if step < nb:
                        stage1(step)
                    if 0 <= step - 1 < nb:
                        stage2(step - 1)
                    if 0 <= step - 2 < nb:
                        stage3(step - 2)
                    pump()
                copy("dve", o_att[:, :, s * 64:(s + 1) * 64], Oacc.rearrange("p (c t) -> p c t", c=4), [("sa", 9)], [("o_att", c) for c in range(4)])

        def run_tile(ti, is_sample):
            N = 128 if is_sample else TT
            nb = N // 128
            t0 = ti * TT
            x_src = xs if is_sample else xp[t0:t0 + N, :]
            p_src = ps_in if is_sample else pp_in[t0:t0 + N, :]
            y_dst = ys if is_sample else yp[t0:t0 + N, :]

            for c in range(8):
                pr, pk = galloc()
                def trf(e, c=c, pr=pr):
                    ins = None
                    for b in range(nb):
                        ins = e.transpose(out=pr[:, b * 128:(b + 1) * 128], in_=xin[:, b, c * 128:(c + 1) * 128], identity=ident_f)
                    return ins
                S.add("pe", trf, reads=["xin", "cst"], writes=[pk], name="xtr")
                copy(evac_eng(), xT[:, c, 0:N], pr[:, 0:N], [pk], [("xT", c)])
            tap(f"xT{ti}", xT[:].rearrange("p a t -> p (a t)"), [128, 8 * TT], XT)

            checkpoint(("S" if is_sample else "") + "A")
            rmsnorm(N, xT[:, :, 0:N], XT, 0, "bf", "B")
            tap(f"nT{ti}", nT[:].rearrange("p a t -> p (a t)"), [128, 8 * TT], NTK) if False else None

            checkpoint(("S" if is_sample else "") + "B")
            for i in range(2):
                def cq(jj, pr, pk, i=i):
                    c = 2 * i + jj
                    act(qz[0:64, c, 0, 0:N], pr[0:64, :], AF.Copy, [pk], [("qz", c)], scale=0.125)
                    act(qz[64:128, c, 1, 0:N], pr[64:128, :], AF.Copy, [pk], [("qz", c)], scale=0.125)
                fm_proj(ti, N, f"q{i}", nT, NTK, 8, 2, cq)
            checkpoint(("S" if is_sample else "") + "C1")
            for i in range(2):
                def ckk(jj, pr, pk, i=i):
                    oc = 2 * i + jj
                    if is_sample:
                        copy(evac_eng(), kTn[:, oc, 0:N], pr, [pk], ["kTn"])
                    else:
                        copy(evac_eng(), kT3[:, oc, t0:t0 + N], pr, [pk], [("kT", ti)])
                fm_proj(ti, N, f"k{i}", nT, NTK, 8, 2, ckk)
                wv, wk = wuse(ti, f"k{i}")
                tok_proj(ti, N, is_sample, wv, wk, 256 * i, None)
            checkpoint(("S" if is_sample else "") + "C2")
            for i in range(2):
                wv, wk = wuse(ti, f"v{i}")
                tok_proj(ti, N, is_sample, wv, wk, 512 + 256 * i, i)
            checkpoint(("S" if is_sample else "") + "C3")
            for i in range(2):
                def cu(jj, pr, pk, i=i):
                    oc = 2 * i + jj
                    copy("dve", uT[:, oc, 0:N], pr, [pk], [("uT", oc), pk])
                    act(du[:, oc, 0:N], pr, AF.Copy, [pk, "d_n"], [("du", oc)], scale=d_n[:, oc:oc + 1])
                fm_proj(ti, N, f"u{i}", nT, NTK, 8, 2, cu)
            checkpoint(("S" if is_sample else "") + "C4")
            if not is_sample:
                if ti + 1 < NT:
                    xload(ti + 1, False)
                elif with_sample:
                    xload(NT, True)
            if is_sample:
                for s in range(2):
                    out_dmas.append(dma("pool", ks[s * 64:(s + 1) * 64, :], kvtok[0:64, s, 0:512], ["kvtok"], []))
                    out_dmas.append(dma("pool", vs[s * 64:(s + 1) * 64, :], kvtok[0:64, s, 512:1024], ["kvtok"], []))
            else:
                out_dmas.append(dma("pool", kp[t0:t0 + N, :].rearrange("(b p) f -> p b f", p=128), kvtok[:, 0:nb, 0:512], ["kvtok"], []))
                out_dmas.append(dma("pool", vp[t0:t0 + N, :].rearrange("(b p) f -> p b f", p=128), kvtok[:, 0:nb, 512:1024], ["kvtok"], []))

            checkpoint(("S" if is_sample else "") + "C")
            if is_sample:
                segs = [(0, 64), (64, 64)]
                def carry_of(si):
                    return lambda j: (carry_s[:, si, j, 0:1], carry_s[:, si, j, 1:2])
                reserved.clear(); reserved.update((3, 4, 5, 6, 7))
                gen = ssm_gen(ti, N, segs, carry_of, ybank=3)
                done = [False]

                def pump():
                    if done[0]:
                        return
                    for _ in range(2):
                        try:
                            next(gen)
                        except StopIteration:
                            done[0] = True
                            return
                attn_sample(ti, pump)
                for _ in gen:
                    pass
                reserved.clear()
                ssm_glu(ti, N)
            else:
                segs = [(0, 128), (128, 128)]
                def carry_of(si):
                    return lambda j: (carry[:, j, 0:1], carry[:, j, 1:2])
                reserved.clear(); reserved.update((4, 5, 6))
                gen = ssm_gen(ti, N, segs, carry_of, carry_on_act=(ti < 4))
                nsteps = NH * ((t0 + N) // 128 + 2)
                per = -(-150 // nsteps)
                done = [False]

                def pump():
                    if done[0]:
                        return
                    for _ in range(per):
                        try:
                            next(gen)
                        except StopIteration:
                            done[0] = True
                            return
                attn_prompt(ti, N, pump)
                for _ in gen:
                    pass
                reserved.clear()
                ssm_glu(ti, N)
            tap(f"ossm{ti}", o_ssm[:].rearrange("p a t -> p (a t)"), [128, 4 * TT], [("o_ssm", c) for c in range(4)])
            tap(f"oatt{ti}", o_att[:].rearrange("p a t -> p (a t)"), [128, 4 * TT], [("o_att", c) for c in range(4)])
            checkpoint(("S" if is_sample else "") + "E")
            dma("act", pst[:, 0:nb, :], p_src.rearrange("(b p) f -> p b f", p=128), [], [("du", 0), ("du", 1)], name="pld")
            OA = [("o_att", c) for c in range(4)]
            OS = [("o_ssm", c) for c in range(4)]
            for half in range(2):
                for pp2 in range(2):
                    p = 2 * half + pp2
                    gav, gak = wuse(ti, f"ga{p}")
                    gsv, gsk = wuse(ti, f"gs{p}")
                    bav, bak = wuse(ti, f"wba{half}")
                    bsv, bsk = wuse(ti, f"wbs{half}")
                    for jj in range(2):
                        oc = 2 * p + jj
                        lc = (oc % 4) * 128
                        pA, kA = galloc()
                        pB, kB = galloc()
                        pC, kC = galloc()
                        pD, kD = galloc()
                        mm_group(pA[:, 0:N], [(gav[:, kc, jj * 128:(jj + 1) * 128], nT[:, kc, 0:N]) for kc in range(8)], [gak] + NTK, [kA], name="ga")
                        mm_group(pB[:, 0:N], [(gsv[:, kc, jj * 128:(jj + 1) * 128], nT[:, kc, 0:N]) for kc in range(8)], [gsk] + NTK, [kB], name="gs")
                        mm_group(pC[:, 0:N], [(bav[:, kc, lc:lc + 128], o_att[:, kc, 0:N]) for kc in range(4)], [bak] + OA, [kC], name="ba")
                        mm_group(pD[:, 0:N], [(bsv[:, kc, lc:lc + 128], o_ssm[:, kc, 0:N]) for kc in range(4)], [bsk] + OS, [kD], name="bs")
                        act(sig[:, 0, 0:N], pA[:, 0:N], AF.Sigmoid, [kA], [("sig", 0)])
                        act(sig[:, 1, 0:N], pB[:, 0:N], AF.Sigmoid, [kB], [("sig", 1)])
                        tt("dve", sig[:, 0, 0:N], pC[:, 0:N], sig[:, 0, 0:N], ALU.mult, [kC, ("sig", 0)], [("sig", 0)])
                        tt("dve", sig[:, 1, 0:N], pD[:, 0:N], sig[:, 1, 0:N], ALU.mult, [kD, ("sig", 1)], [("sig", 1)])
                        tt("dve", mg[:, oc, 0:N], sig[:, 0, 0:N], sig[:, 1, 0:N], ALU.add, [("sig", 0), ("sig", 1)], [("mg", oc)])
            MG = [("mg", c) for c in range(8)]
            for i in range(4):
                def cwo(jj, pr, pk, i=i):
                    oc = 2 * i + jj
                    copy(evac_eng(), fout[:, oc, 0:N], pr, [pk], [("fout", oc)])
                fm_proj(ti, N, f"wo{i}", mg, MG, 8, 2, cwo)
            rmsnorm(N, fout[:, :, 0:N], FO, 1, "res", "F")
            tap(f"x1_{ti}", xT[:].rearrange("p a t -> p (a t)"), [128, 8 * TT], XT)

            checkpoint(("S" if is_sample else "") + "F")
            rmsnorm(N, xT[:, :, 0:N], XT, 2, "bf", "G")
            hid = R[:, 0:11, :].rearrange("p a t -> p (a t)").bitcast(BF16).rearrange("p (c t) -> p c t", c=22)

            def hk(ocf):
                return ("R", ocf // 2)
            for i in range(11):
                gv, gk = wuse(ti, f"fg{i}")
                uv, uk = wuse(ti, f"fu{i}")
                for jj in range(2):
                    ocf = 2 * i + jj
                    pA, kA = galloc()
                    pB, kB = galloc()
                    mm_group(pA[:, 0:N], [(gv[:, kc, jj * 128:(jj + 1) * 128], nT[:, kc, 0:N]) for kc in range(8)], [gk] + NTK, [kA], name="fg")
                    mm_group(pB[:, 0:N], [(uv[:, kc, jj * 128:(jj + 1) * 128], nT[:, kc, 0:N]) for kc in range(8)], [uk] + NTK, [kB], name="fu")
                    sl = ocf % 2
                    act(sig[:, sl, 0:N], pA[:, 0:N], AF.Silu, [kA], [("sig", sl)])
                    tt("dve", hid[:, ocf, 0:N], pB[:, 0:N], sig[:, sl, 0:N], ALU.mult, [kB, ("sig", sl)], [hk(ocf)])
            reserved.clear(); reserved.update((6, 7))
            for cb in range(4):
                for kg, (k0, kn) in enumerate(((0, 8), (8, 8), (16, 6))):
                    dv, dk = wuse(ti, f"fd{cb}_{kg}")
                    for jj in range(2):
                        bank = pbank[6 + jj]
                        mm_group(bank[:, 0:N], [(dv[:, kc, jj * 128:(jj + 1) * 128], hid[:, k0 + kc, 0:N]) for kc in range(kn)],
                                 [dk] + [("R", r) for r in range(11)], [("PB", 6 + jj)], start=(kg == 0), stop=(kg == 2), name="fd")
                for jj in range(2):
                    oc = 2 * cb + jj
                    copy(evac_eng(), fout[:, oc, 0:N], pbank[6 + jj][:, 0:N], [("PB", 6 + jj)], [("fout", oc)])
            reserved.clear()
            rmsnorm(N, fout[:, :, 0:N], FO, 3, "res", "G2")
            tap(f"x2_{ti}", xT[:].rearrange("p a t -> p (a t)"), [128, 8 * TT], XT)

            checkpoint(("S" if is_sample else "") + "G")
            rmsnorm(N, xT[:, :, 0:N], XT, 4, "bf", "H")
            for c2 in range(2):
                pr, pk = galloc()
                def trp(e, c2=c2, pr=pr):
                    ins = None
                    for b in range(nb):
                        ins = e.transpose(out=pr[:, b * 128:(b + 1) * 128], in_=pst[:, b, c2 * 128:(c2 + 1) * 128], identity=ident_f)
                    return ins
                S.add("pe", trp, reads=[("du", 0), ("du", 1), "cst"], writes=[pk], name="ptr")
                copy(evac_eng(), pT[:, c2, 0:N], pr[:, 0:N], [pk], [("pT", c2)])
            ppv = None
            for i in range(4):
                gv, gk = wuse(ti, f"pg{i}")
                if i == 0:
                    pass
                for jj in range(2):
                    oc = 2 * i + jj
                    pA, kA = galloc()
                    mm_group(pA[:, 0:N], [(gv[:, kc, jj * 128:(jj + 1) * 128], nT[:, kc, 0:N]) for kc in range(8)], [gk] + NTK, [kA], name="pg")
                    act(fout[:, oc, 0:N], pA[:, 0:N], AF.Sigmoid, [kA], [("fout", oc)])
            ppv, ppk = wuse(ti, "pp")
            for oc in range(8):
                pB, kB = galloc()
                mm_group(pB[:, 0:N], [(ppv[:, kc, oc * 128:(oc + 1) * 128], pT[:, kc, 0:N]) for kc in range(2)], [ppk, ("pT", 0), ("pT", 1)], [kB], name="pp")
                tt("dve", fout[:, oc, 0:N], pB[:, 0:N], fout[:, oc, 0:N], ALU.mult, [kB, ("fout", oc)], [("fout", oc)])
            rmsnorm(N, fout[:, :, 0:N], FO, 5, "res", "H2")

            checkpoint(("S" if is_sample else "") + "H")
            for b in range(nb):
                for half in range(2):
                    pr0, pk0 = galloc()
                    pr1, pk1 = galloc()
                    for (pr, pk, cbase) in ((pr0, pk0, 4 * half), (pr1, pk1, 4 * half + 2)):
                        def try_(e, pr=pr, cbase=cbase, b=b):
                            ins = None
                            for cc in range(2):
                                ins = e.transpose(out=pr[:, cc * 128:(cc + 1) * 128], in_=xT[:, cbase + cc, b * 128:(b + 1) * 128], identity=ident_f)
                            return ins
                        S.add("pe", try_, reads=[("xT", cbase), ("xT", cbase + 1), "cst"], writes=[pk], name="ytr")
                        copy(evac_eng(), kvtok[:, b, cbase * 128:(cbase + 2) * 128], pr[:, 0:256], [pk], ["kvtok"])
            out_dmas.append(dma("pool", y_dst.rearrange("(b p) f -> p b f", p=128), kvtok[:, 0:nb, :], ["kvtok"], [], name="yst"))

        def xload(ti, is_sample):
            N = 128 if is_sample else TT
            x_src = xs if is_sample else xp[ti * TT:ti * TT + N, :]
            dma("act", xin[:, 0:N // 128, :], x_src.rearrange("(b p) f -> p b f", p=128), [], ["xin"], name="xld")

        def tok_proj(ti, N, is_sample, wv, wk, col0, vidx):
            t0 = ti * TT
            if is_sample:
                for s in range(2):
                    pr, pk = galloc()
                    mm_group(pr[0:64, 0:256], [(nT[:, kc, s * 64:(s + 1) * 64], wv[:, kc, 0:256]) for kc in range(8)], [wk] + NTK, [pk], name="tokp")
                    copy("dve", kvtok[0:64, s, col0:col0 + 256], pr[0:64, 0:256], [pk], ["kvtok", pk])
                    if vidx is not None:
                        copy("act", v_all[0:64, s * 512 + 256 * vidx:s * 512 + 256 * (vidx + 1)], pr[0:64, 0:256], [pk], ["vnew"])
            else:
                for b in range(N // 128):
                    pr, pk = galloc()
                    mm_group(pr[:, 0:256], [(nT[:, kc, b * 128:(b + 1) * 128], wv[:, kc, 0:256]) for kc in range(8)], [wk] + NTK, [pk], name="tokp")
                    copy("dve", kvtok[:, b, col0:col0 + 256], pr[:, 0:256], [pk], ["kvtok", pk])
                    if vidx is not None:
                        blk = t0 // 128 + b
                        copy("act", v3[:, blk, 256 * vidx:256 * (vidx + 1)], pr[:, 0:256], [pk], [("v", blk)])

        xload(0, False)
        for ti in range(NT):
            run_tile(ti, False)
        out_dmas.append(dma("pool", stp[:, :, :], carry[:], ["s:carry"] + [("s:carry", 0, j) for j in range(16)] + [("s:carry", 1, j) for j in range(16)], []))
        if with_sample:
            run_tile(NT, True)
            out_dmas.append(dma("pool", sts[:, :, :, :], carry_s[:], ["s:carry_s"] + [("s:carry", si, j) for si in range(2) for j in range(16)], []))


    except _Stop:
        pass

    S.barrier_wait("sp", out_dmas, name="final")

    sems = {e: es.enter_context(nc.semaphore("sem_" + e)) for e in ENGINES}
    dma_sems = {}
    for e in ENGINES:
        for i in range(min(NDMA_RING[e], S.dma_count[e])):
            dma_sems[(e, i)] = es.enter_context(nc.semaphore(f"dsem_{e}_{i}"))
    block = es.enter_context(nc.Block())
    S.assign()

    @block.tensor
    def _(e):
        S.emit_engine("pe", e, sems, dma_sems)

    @block.scalar
    def _(e):
        S.emit_engine("act", e, sems, dma_sems)

    @block.vector
    def _(e):
        S.emit_engine("dve", e, sems, dma_sems)

    @block.gpsimd
    def _(e):
        S.emit_engine("pool", e, sems, dma_sems)

    @block.sync
    def _(e):
        S.emit_engine("sp", e, sems, dma_sems)

    es.close()
    return nc, S


def n_layout(a):
    return np.ascontiguousarray(a.reshape(16, 128).T)


def prep_shared(inp):
    f = np.float32
    sh = {}
    sh["w_in"] = np.ascontiguousarray(inp["w_in"][0], dtype=f)
    sh["w_glu"] = np.ascontiguousarray(inp["w_glu"][0], dtype=f)
    sh["w_ba"] = np.ascontiguousarray(inp["w_branch_attn"][0], dtype=f)
    sh["w_bs"] = np.ascontiguousarray(inp["w_branch_ssm"][0], dtype=f)
    sh["w_out"] = np.ascontiguousarray(inp["w_out"][0], dtype=f)
    sh["w_fg"] = np.ascontiguousarray(inp["w_ffn_gate"][0], dtype=f)
    sh["w_fu"] = np.ascontiguousarray(inp["w_ffn_up"][0], dtype=f)
    sh["w_fd"] = np.ascontiguousarray(inp["w_ffn_down"][0], dtype=f)
    sh["w_pg"] = np.ascontiguousarray(inp["w_ple_gate"][0], dtype=f)
    sh["w_pp"] = np.ascontiguousarray(inp["w_ple_proj"][0], dtype=f)
    gl = [inp[k][0] for k in ("norm_mix_pre", "norm_mix_post", "norm_ffn_pre", "norm_ffn_post", "norm_ple_pre", "norm_ple_post")]
    sh["gains"] = np.ascontiguousarray(np.concatenate([np.asarray(g, f).reshape(8, 128).T for g in gl], axis=1))
    a_re = np.asarray(inp["ssm_a_re"][0], f)
    a_im = np.asarray(inp["ssm_a_im"][0], f)
    ldt = np.repeat(np.asarray(inp["ssm_log_dt"][0], f)[:, None], 64, axis=1)
    sh["ssm_n"] = np.ascontiguousarray(np.stack([n_layout(a_re), n_layout(a_im), n_layout(ldt)], axis=1))
    bpad = np.zeros((128, 2, 4, 512), f)
    for comp, key in enumerate(("ssm_b_re", "ssm_b_im")):
        b = np.asarray(inp[key][0], f)
        for g in range(32):
            m, gl_ = g // 8, g % 8
            bpad[gl_ * 16:(gl_ + 1) * 16, comp, m, gl_ * 64:(gl_ + 1) * 64] = b[g].T
    sh["bpad"] = bpad
    cpad = np.zeros((128, 2, 16, 128), f)
    for comp, key in enumerate(("ssm_c_re", "ssm_c_im")):
        cm = np.asarray(inp[key][0], f)
        for g in range(32):
            j, g2, gl_ = g // 2, g % 2, g % 8
            cpad[g2 * 64:(g2 + 1) * 64, comp, j, gl_ * 16:(gl_ + 1) * 16] = cm[g].T
    sh["cpad"] = cpad
    sh["d_n"] = np.ascontiguousarray(np.asarray(inp["ssm_d"][0], f).reshape(4, 128).T)
    cst = np.zeros((128, 4, 128), f)
    cst[:, 0, :] = np.eye(128, dtype=f)
    jj, ss = np.meshgrid(np.arange(128), np.arange(128), indexing="ij")
    cst[:, 1, :] = np.where(jj >= ss, -1.0, 0.0)
    cst[:, 2, :] = -1.0
    cst[:, 3, :] = np.where(jj < ss, 1.0, 0.0)
    sh["cst"] = cst
    return sh


def prep_core(inp, c):
    f = np.float32
    m = {}
    m["xp"] = np.ascontiguousarray(inp["x_prompt"][c], dtype=f)
    m["xs"] = np.ascontiguousarray(np.asarray(inp["x_sample"][2 * c:2 * c + 2], f).reshape(128, D))
    m["pp"] = np.ascontiguousarray(inp["p_prompt"][0, c], dtype=f)
    m["psm"] = np.ascontiguousarray(np.asarray(inp["p_sample"][0, 2 * c:2 * c + 2], f).reshape(128, PLE))
    m["ck"] = np.ascontiguousarray(np.asarray(inp["cache_k"][0, 2 * c:2 * c + 2], f).reshape(2, PAST, 512))
    m["cv"] = np.ascontiguousarray(np.asarray(inp["cache_v"][0, 2 * c:2 * c + 2], f).reshape(2, PAST, 512))
    s0 = np.zeros((128, 2, 2, 16), f)
    for s in range(2):
        s0[:, s, 0, :] = n_layout(np.asarray(inp["state_ssm_re"][0, 2 * c + s], f))
        s0[:, s, 1, :] = n_layout(np.asarray(inp["state_ssm_im"][0, 2 * c + s], f))
    m["s0"] = s0
    return m


def from_n_layout(a):
    return np.ascontiguousarray(a.T).reshape(32, 64)


_CACHE = {}


def kernel(**inputs):
    inp = {k: np.asarray(v) for k, v in inputs.items()}
    if "nc" not in _CACHE:
        _CACHE["nc"] = build_program()[0]
    nc = _CACHE["nc"]
    sh = prep_shared(inp)
    in_maps = []
    for c in range(NCORES):
        m = dict(sh)
        m.update(prep_core(inp, c))
        in_maps.append(m)
    res = run_bass_kernel_spmd(nc, in_maps, core_ids=list(range(NCORES)))
    R_ = res.results
    f = np.float32
    y_prompt = np.stack([R_[c]["yp"] for c in range(NCORES)]).astype(f)
    y_sample = np.concatenate([R_[c]["ys"].reshape(2, DSEQ, D) for c in range(NCORES)]).astype(f)
    k_prompt = np.stack([R_[c]["kp"].reshape(SEQ, NH, HD) for c in range(NCORES)])[None].astype(f)
    v_prompt = np.stack([R_[c]["vp"].reshape(SEQ, NH, HD) for c in range(NCORES)])[None].astype(f)
    k_sample = np.concatenate([R_[c]["ks"].reshape(2, DSEQ, NH, HD) for c in range(NCORES)])[None].astype(f)
    v_sample = np.concatenate([R_[c]["vs"].reshape(2, DSEQ, NH, HD) for c in range(NCORES)])[None].astype(f)
    srp = np.stack([from_n_layout(R_[c]["stp"][:, :, 0]) for c in range(NCORES)])[None].astype(f)
    sip = np.stack([from_n_layout(R_[c]["stp"][:, :, 1]) for c in range(NCORES)])[None].astype(f)
    srs = np.stack([from_n_layout(R_[c]["sts"][:, s, :, 0]) for c in range(NCORES) for s in range(2)])[None].astype(f)
    sis = np.stack([from_n_layout(R_[c]["sts"][:, s, :, 1]) for c in range(NCORES) for s in range(2)])[None].astype(f)
    return (y_prompt, y_sample, k_prompt, v_prompt, srp, sip, k_sample, v_sample, srs, sis)
```
